# Optimizing a Trainium2 kernel written in Bass

```python
import jax, jax.numpy as jnp
from jax import lax
import numpy as np

D_MODEL = 1024
BATCH = 8
SEQ = 4096
DEPTH = 1

PLE_DIM = 256
MLA_HEADS = 8
QK_NOPE_DIM = 64
QK_ROPE_DIM = 32
V_HEAD_DIM = 64
Q_LORA_RANK = 384
KV_LORA_RANK = 256
ROPE_THETA = 10000.0
Q_BLOCK = 128
CONV_CHANNELS = 512
CONV_WIDTH = 31
MLA_WIDTH = MLA_HEADS * V_HEAD_DIM
D_MIX = MLA_WIDTH + CONV_CHANNELS
IN_PROJ_DIM = Q_LORA_RANK + KV_LORA_RANK + QK_ROPE_DIM + 2 * CONV_CHANNELS
D_FF = 2816
FFN_CONV_WIDTH = 3
NORM_EPS = 1e-6

kernel_name = "hymba_mla_conformer_convffn_sandwich_ple"


def rms_norm(x, g):
    xf = x.astype(jnp.float32)
    y = xf * lax.rsqrt(jnp.mean(xf * xf, axis=-1, keepdims=True) + NORM_EPS)
    return (y * g.astype(jnp.float32)).astype(x.dtype)


def layer_norm(x, g, b):
    xf = x.astype(jnp.float32)
    mu = jnp.mean(xf, axis=-1, keepdims=True)
    xc = xf - mu
    y = xc * lax.rsqrt(jnp.mean(xc * xc, axis=-1, keepdims=True) + NORM_EPS)
    return (y * g.astype(jnp.float32) + b.astype(jnp.float32)).astype(x.dtype)


def causal_depthwise_conv(x, w, b):
    k = w.shape[0]
    out = lax.conv_general_dilated(
        x, w[:, None, :].astype(x.dtype), window_strides=(1,), padding=((k - 1, 0),),
        dimension_numbers=('NWC', 'WIO', 'NWC'), feature_group_count=x.shape[-1])
    return out + b.astype(x.dtype)


def apply_rope(x, cos, sin):
    x1, x2 = jnp.split(x, 2, axis=-1)
    return jnp.concatenate([x1 * cos - x2 * sin, x1 * sin + x2 * cos], axis=-1)


def causal_mla_attention(q_nope, q_rope, k_nope, k_rope, v):
    b, s, h, _ = q_nope.shape
    nb = s // Q_BLOCK
    scale = (QK_NOPE_DIM + QK_ROPE_DIM) ** -0.5
    k_idx = jnp.arange(s)

    def to_blocks(t):
        return jnp.moveaxis(t.reshape(b, nb, Q_BLOCK, *t.shape[2:]), 1, 0)

    def block(args):
        qn, qr, start = args
        sc = (jnp.einsum('bqhd,bkhd->bhqk', qn, k_nope, preferred_element_type=jnp.float32)
              + jnp.einsum('bqhr,bkr->bhqk', qr, k_rope, preferred_element_type=jnp.float32)) * scale
        q_idx = start + jnp.arange(Q_BLOCK)
        sc = jnp.where(k_idx[None, :] <= q_idx[:, None], sc, -jnp.inf)
        pr = jax.nn.softmax(sc, axis=-1).astype(v.dtype)
        return jnp.einsum('bhqk,bkhd->bqhd', pr, v)

    starts = jnp.arange(nb) * Q_BLOCK
    out = lax.map(block, (to_blocks(q_nope), to_blocks(q_rope), starts))
    return jnp.moveaxis(out, 0, 1).reshape(b, s, h * V_HEAD_DIM)


def hybrid_mixer(xn, cos, sin, w_in, g_q_a, w_q_b, g_kv_a, w_kv_b,
                 conv_w, conv_b, conv_ln_g, conv_ln_b, w_o):
    b, s, _ = xn.shape
    proj = xn @ w_in
    o1 = Q_LORA_RANK
    o2 = o1 + KV_LORA_RANK
    o3 = o2 + QK_ROPE_DIM
    q_a, kv_a, k_rope, conv_in = jnp.split(proj, [o1, o2, o3], axis=-1)
    q = (rms_norm(q_a, g_q_a) @ w_q_b).reshape(b, s, MLA_HEADS, QK_NOPE_DIM + QK_ROPE_DIM)
    q_nope, q_rope = jnp.split(q, [QK_NOPE_DIM], axis=-1)
    kv = (rms_norm(kv_a, g_kv_a) @ w_kv_b).reshape(b, s, MLA_HEADS, QK_NOPE_DIM + V_HEAD_DIM)
    k_nope, v = jnp.split(kv, [QK_NOPE_DIM], axis=-1)
    q_rope = apply_rope(q_rope, cos[:, :, None, :], sin[:, :, None, :])
    k_rope = apply_rope(k_rope, cos, sin)
    attn = causal_mla_attention(q_nope, q_rope, k_nope, k_rope, v)
    a, gate = jnp.split(conv_in, 2, axis=-1)
    c = a * jax.nn.sigmoid(gate)
    c = causal_depthwise_conv(c, conv_w, conv_b)
    c = jax.nn.silu(layer_norm(c, conv_ln_g, conv_ln_b))
    return jnp.concatenate([attn, c], axis=-1) @ w_o


def conv_gated_ffn(xn, w_gate, w_up, dw_w, dw_b, w_down):
    g = causal_depthwise_conv(xn @ w_gate, dw_w, dw_b)
    return (jax.nn.gelu(g, approximate=True) * (xn @ w_up)) @ w_down


def setup_inputs(seed: int = 0) -> dict:
    key = jax.random.key(seed)
    ks = iter(jax.random.split(key, 40))

    def w(shape, fan_in):
        return jax.random.normal(next(ks), shape, jnp.float32) * fan_in ** -0.5

    def gain(shape):
        return 1.0 + 0.05 * jax.random.normal(next(ks), shape, jnp.float32)

    def bias(shape):
        return 0.01 * jax.random.normal(next(ks), shape, jnp.float32)

    L = DEPTH
    x = jax.random.normal(next(ks), (BATCH, SEQ, D_MODEL), jnp.float32)
    p = jax.random.normal(next(ks), (DEPTH, BATCH, SEQ, PLE_DIM), jnp.float32)
    positions = jnp.broadcast_to(jnp.arange(SEQ, dtype=jnp.int32)[None, :], (BATCH, SEQ))
    return {
        "x": x,
        "p": p,
        "positions": positions,
        "g_mix_pre": gain((L, D_MODEL)),
        "w_in": w((L, D_MODEL, IN_PROJ_DIM), D_MODEL),
        "g_q_a": gain((L, Q_LORA_RANK)),
        "w_q_b": w((L, Q_LORA_RANK, MLA_HEADS * (QK_NOPE_DIM + QK_ROPE_DIM)), Q_LORA_RANK),
        "g_kv_a": gain((L, KV_LORA_RANK)),
        "w_kv_b": w((L, KV_LORA_RANK, MLA_HEADS * (QK_NOPE_DIM + V_HEAD_DIM)), KV_LORA_RANK),
        "conv_w": w((L, CONV_WIDTH, CONV_CHANNELS), CONV_WIDTH),
        "conv_b": bias((L, CONV_CHANNELS)),
        "conv_ln_g": gain((L, CONV_CHANNELS)),
        "conv_ln_b": bias((L, CONV_CHANNELS)),
        "w_o": w((L, D_MIX, D_MODEL), D_MIX),
        "g_mix_post": gain((L, D_MODEL)),
        "g_ffn_pre": gain((L, D_MODEL)),
        "w_ffn_gate": w((L, D_MODEL, D_FF), D_MODEL),
        "w_ffn_up": w((L, D_MODEL, D_FF), D_MODEL),
        "ffn_dw_w": w((L, FFN_CONV_WIDTH, D_FF), FFN_CONV_WIDTH),
        "ffn_dw_b": bias((L, D_FF)),
        "w_ffn_down": w((L, D_FF, D_MODEL), D_FF),
        "g_ffn_post": gain((L, D_MODEL)),
        "w_ple_proj": w((L, PLE_DIM, D_MODEL), PLE_DIM),
        "g_ple": gain((L, D_MODEL)),
        "w_ple_gate": w((L, D_MODEL, D_MODEL), D_MODEL),
    }


def reference(x, p, positions, g_mix_pre, w_in, g_q_a, w_q_b, g_kv_a, w_kv_b,
              conv_w, conv_b, conv_ln_g, conv_ln_b, w_o, g_mix_post, g_ffn_pre,
              w_ffn_gate, w_ffn_up, ffn_dw_w, ffn_dw_b, w_ffn_down, g_ffn_post,
              w_ple_proj, g_ple, w_ple_gate):
    inv_freq = ROPE_THETA ** (-jnp.arange(0, QK_ROPE_DIM, 2, dtype=jnp.float32) / QK_ROPE_DIM)
    ang = positions.astype(jnp.float32)[..., None] * inv_freq
    cos = jnp.cos(ang).astype(x.dtype)
    sin = jnp.sin(ang).astype(x.dtype)

    h = x
    for i in range(DEPTH):
        mix = hybrid_mixer(rms_norm(h, g_mix_pre[i]), cos, sin, w_in[i], g_q_a[i], w_q_b[i],
                           g_kv_a[i], w_kv_b[i], conv_w[i], conv_b[i], conv_ln_g[i],
                           conv_ln_b[i], w_o[i])
        h = h + rms_norm(mix, g_mix_post[i])
        ffn = conv_gated_ffn(rms_norm(h, g_ffn_pre[i]), w_ffn_gate[i], w_ffn_up[i],
                             ffn_dw_w[i], ffn_dw_b[i], w_ffn_down[i])
        h = h + rms_norm(ffn, g_ffn_post[i])
        e = rms_norm(p[i] @ w_ple_proj[i], g_ple[i])
        h = h + jax.nn.sigmoid(h @ w_ple_gate[i]) * e
    return h
```

```python
import contextlib
import numpy as np
import concourse.bass as bass
import concourse.mybir as mybir
from concourse.ap import AP
from concourse.bass_utils import run_bass_kernel_spmd

F32 = mybir.dt.float32
BF16 = mybir.dt.bfloat16
I32 = mybir.dt.int32
AF = mybir.ActivationFunctionType
ALU = mybir.AluOpType
AX = mybir.AxisListType

SEQ = 4096
DM = 1024
T = 256
NT = SEQ // T
NSUB = T // 128
EPS = 1e-6
ATT_SCALE = 96.0 ** -0.5
DFF = 2816
NFC = DFF // 128
NSLOT = 12
MASKVAL = -30000.0

COMPUTE = ("pe", "act", "dve", "pool")


class Res:
    _n = 0

    def __init__(self, name, nsub=1):
        self.name = name
        self.nsub = nsub
        self.id = Res._n
        Res._n += 1
        self.exclusive = False

    def keys(self):
        return [(self.id, i) for i in range(self.nsub)]

    def __getitem__(self, i):
        return SubRes(self, i)


class SubRes:
    def __init__(self, res, i):
        self.res = res
        self.i = i
        self.exclusive = res.exclusive

    def keys(self):
        return [(self.res.id, self.i)]


class Op:
    __slots__ = ("idx", "eng", "fn", "is_dma", "key", "deps", "sig", "sigval", "dma_sem",
                 "dma_target", "name", "snap", "chain")


class Sched:
    def __init__(self, nc, same_engine_sync=False):
        self.nc = nc
        self.ops = []
        self.last_writers = {}
        self.readers = {}
        self.same_engine_sync = same_engine_sync

    def _add(self, eng, fn, reads, writes, is_dma=False, key=None, name=""):
        op = Op()
        op.idx = len(self.ops)
        op.eng = eng
        op.fn = fn
        op.is_dma = is_dma
        op.key = key
        op.name = name
        op.sig = False
        op.chain = name == "chain"
        op.name = getattr(self, "cur", "")
        op.snap = [(n, c.cell_contents) for n, c in zip(fn.__code__.co_freevars, fn.__closure__ or ())]
        deps = set()
        rk = []
        wk = []
        for r in reads:
            if r.exclusive:
                wk += r.keys()
            else:
                rk += r.keys()
        for w in writes:
            wk += w.keys()
        for k in rk:
            for w in self.last_writers.get(k, ()):
                deps.add(w)
        for k in wk:
            for w in self.last_writers.get(k, ()):
                deps.add(w)
            for r in self.readers.get(k, ()):
                deps.add(r)
        op.deps = deps
        for k in rk:
            self.readers.setdefault(k, []).append(op.idx)
        for k in wk:
            self.last_writers[k] = [op.idx]
            self.readers[k] = []
        self.ops.append(op)
        return op

    def pe(self, fn, reads=(), writes=(), name=""):
        return self._add("pe", fn, reads, writes, name=name)

    def act(self, fn, reads=(), writes=(), name=""):
        return self._add("act", fn, reads, writes, name=name)

    def dve(self, fn, reads=(), writes=(), name=""):
        return self._add("dve", fn, reads, writes, name=name)

    def pool(self, fn, reads=(), writes=(), name=""):
        return self._add("pool", fn, reads, writes, name=name)

    def dma(self, queue, fn, reads=(), writes=(), key=None, name=""):
        return self._add(queue, fn, reads, writes, is_dma=True, key=key, name=name)

    def _skip(self, dop, op):
        if dop.is_dma or op.is_dma:
            return False
        if dop.eng != op.eng:
            return False
        if dop.eng == "pe":
            return True
        if dop.chain and op.chain and self.same_engine_sync is not True:
            return True
        ses = self.same_engine_sync
        if isinstance(ses, (set, frozenset, list, tuple)):
            return dop.eng not in ses
        return not ses

    def emit(self, final_wait_ops=()):
        nc = self.nc
        ops = self.ops
        for op in ops:
            for d in op.deps:
                dop = ops[d]
                if dop.is_dma or self._skip(dop, op):
                    continue
                dop.sig = True
        for op in final_wait_ops:
            if not op.is_dma:
                op.sig = True
        cnt = {e: 0 for e in COMPUTE}
        for op in ops:
            if not op.is_dma and op.sig:
                cnt[op.eng] += 1
                op.sigval = cnt[op.eng]
        stack = contextlib.ExitStack()
        esem = {e: stack.enter_context(nc.semaphore("s_" + e)) for e in COMPUTE}
        dsem = {}
        dcnt = {}
        for op in ops:
            if op.is_dma:
                kid = op.key.id
                if kid not in dsem:
                    dsem[kid] = stack.enter_context(nc.semaphore("d_%s" % op.key.name))
                    dcnt[kid] = 0
                dcnt[kid] += 16
                op.dma_sem = dsem[kid]
                op.dma_target = dcnt[kid]
        self.n_sems = len(esem) + len(dsem)
        streams = {"pe": [], "act": [], "dve": [], "pool": [], "sp": []}
        for op in ops:
            streams[op.eng].append(op)
        block = stack.enter_context(nc.Block())

        def run_stream(eng_name, eng):
            seen = {}
            for op in streams[eng_name]:
                waits = {}
                for d in op.deps:
                    dop = ops[d]
                    if dop.is_dma:
                        s, v = dop.dma_sem, dop.dma_target
                    else:
                        if self._skip(dop, op):
                            continue
                        s, v = esem[dop.eng], dop.sigval
                    if seen.get(s.num, 0) >= v:
                        continue
                    if waits.get(s.num, (None, 0))[1] < v:
                        waits[s.num] = (s, v)
                for s, v in waits.values():
                    eng.wait_ge(s, v)
                    seen[s.num] = v
                for (n, v0), c in zip(op.snap, op.fn.__closure__ or ()):
                    v1 = c.cell_contents
                    if not (v1 is v0 or (isinstance(v0, (int, float, bool, str, tuple)) and v0 == v1)):
                        raise RuntimeError("late-bound closure variable %r changed in op %d (%s)" % (n, op.idx, op.fn.__code__.co_firstlineno))
                ins = op.fn(eng)
                if op.is_dma:
                    ins.then_inc(op.dma_sem, 16)
                elif op.sig:
                    ins.then_inc(esem[op.eng], 1)
            if eng_name == "sp":
                for op in final_wait_ops:
                    if op.is_dma:
                        eng.wait_ge(op.dma_sem, op.dma_target)
                    else:
                        eng.wait_ge(esem[op.eng], op.sigval)

        @block.tensor
        def _(e):
            run_stream("pe", e)

        @block.scalar
        def _(e):
            run_stream("act", e)

        @block.vector
        def _(e):
            run_stream("dve", e)

        @block.gpsimd
        def _(e):
            run_stream("pool", e)

        @block.sync
        def _(e):
            run_stream("sp", e)

        stack.close()


V_GPRE, V_GQA, V_GKVA, V_GFFN, V_CB, V_LNG, V_LNB, V_CW, V_FW, V_FB, NV = 0, 8, 11, 13, 21, 25, 29, 33, 157, 223, 245


def _fm(v):
    v = np.asarray(v, np.float32)
    return np.ascontiguousarray(v.reshape(-1, 128).T)


def pack_vecs(inp):
    vec = np.zeros((128, 256), np.float32)
    vec[:, V_GPRE:V_GPRE + 8] = _fm(inp["g_mix_pre"][0])
    vec[:, V_GQA:V_GQA + 3] = _fm(inp["g_q_a"][0])
    vec[:, V_GKVA:V_GKVA + 2] = _fm(inp["g_kv_a"][0])
    vec[:, V_GFFN:V_GFFN + 8] = _fm(inp["g_ffn_pre"][0])
    vec[:, V_CB:V_CB + 4] = _fm(inp["conv_b"][0])
    vec[:, V_LNG:V_LNG + 4] = _fm(inp["conv_ln_g"][0])
    vec[:, V_LNB:V_LNB + 4] = _fm(inp["conv_ln_b"][0])
    cw = np.asarray(inp["conv_w"][0], np.float32)
    for cg in range(4):
        vec[:, V_CW + cg * 31:V_CW + (cg + 1) * 31] = cw[:, cg * 128:(cg + 1) * 128].T
    fw = np.asarray(inp["ffn_dw_w"][0], np.float32)
    for c in range(NFC):
        vec[:, V_FW + c * 3:V_FW + (c + 1) * 3] = fw[:, c * 128:(c + 1) * 128].T
    vec[:, V_FB:V_FB + NFC] = _fm(inp["ffn_dw_b"][0])
    return vec


def make_consts():
    c = np.zeros((128, 512), np.float32)
    c[:, 0:128] = np.eye(128, dtype=np.float32)
    k = np.arange(128)[:, None]
    q = np.arange(128)[None, :]
    c[:, 128:256] = np.where(k <= q, 0.0, MASKVAL).astype(np.float32)
    inv_freq = (10000.0 ** (-np.arange(0, 32, 2, dtype=np.float32) / 32.0)).astype(np.float32)
    c[:, 256:272] = inv_freq[None, :]
    return c


P_WINT = 0
P_WQB = 8
P_WKVB = 11
P_WINF = 13
P_WO = 21
P_WGU = 29
P_WD = 73
P_WPP = 95
P_WPG = 97
NPIECE = 105


def build_nc(ntiles=NT, dbg=None, same_engine_sync=False, stop=None, noconv=False):
    dbg = dbg or {}
    nc = bass.Bass("TRN2", target_bir_lowering=False)
    S = Sched(nc, same_engine_sync=same_engine_sync)

    def din(name, shape, dt=F32):
        return nc.dram_tensor(name, list(shape), dt, kind="ExternalInput").ap()

    x_d = din("x", [SEQ, DM])
    p_d = din("p", [SEQ, 256])
    pos_d = din("pos", [32, 128], I32)
    w_in_d = din("w_in", [1024, 1696])
    w_qb_d = din("w_q_b", [384, 768])
    w_kvb_d = din("w_kv_b", [256, 1024])
    w_o_d = din("w_o", [1024, 1024])
    w_g_d = din("w_ffn_gate", [1024, DFF])
    w_u_d = din("w_ffn_up", [1024, DFF])
    w_d_d = din("w_ffn_down", [DFF, 1024])
    w_pp_d = din("w_ple_proj", [256, 1024])
    w_pg_d = din("w_ple_gate", [1024, 1024])
    vec_d = din("vecs", [128, 256])
    grow_d = din("grow", [128, 3, 1024])
    const_d = din("consts", [128, 512])
    y_d = nc.dram_tensor("y", [SEQ, DM], F32, kind="ExternalOutput").ap()
    ws_d = nc.dram_tensor("wscratch", [NPIECE, 128, 1024], BF16, kind="Internal").ap()
    dbg_d = {k: nc.dram_tensor("dbg_" + k, list(shp), F32, kind="ExternalOutput").ap()
             for k, shp in dbg.items()}

    off = [16512]

    def sb(name, shape, dt, at=None):
        n = 1
        for s in shape[1:]:
            n *= s
        nb = n * (4 if dt in (F32, I32) else 2)
        nb = (nb + 31) // 32 * 32
        if at is None:
            at = off[0]
            off[0] += nb
        assert at + nb <= 229344, (name, at, nb)
        return nc.alloc_sbuf_tensor_at(name, list(shape), dt, offset=at).ap()

    KT = sb("KT", [98, 8, SEQ], BF16)
    VP = sb("VP", [128, 32, 8, 65], BF16)
    GROW = sb("GROW", [128, 3, 1024], F32)
    RING = [sb("RING%d" % i, [128, 1024], BF16) for i in range(NSLOT)]
    VEC = sb("VEC", [128, 256], F32)
    CST = sb("CST", [128, 512], F32)
    IDB = sb("IDB", [128, 128], BF16)
    MASK = sb("MASK", [128, 128], BF16)
    ONESB = sb("ONESB", [128, 128], BF16)
    ONESF = sb("ONESF", [128, 64], F32)
    SEL = sb("SEL", [128, 64], BF16)
    COS = sb("COS", [128, 32, 16], F32)
    SIN = sb("SIN", [128, 32, 16], F32)
    NSIN = sb("NSIN", [128, 32, 16], F32)
    POSF = sb("POSF", [128, 32], F32)
    GH = sb("GH", [128, NFC, 2], F32)
    SM = sb("SM", [128, 64], F32)
    H = sb("H", [128, NSUB, 1024], F32)
    XNT = sb("XNT", [128, 8, T], BF16)
    XB = sb("XB", [128, NSUB, 1024], BF16)
    JUNK = sb("JUNK", [128, 1024], BF16)
    TMPF = sb("TMPF", [128, 1024], F32)
    AUGQ = sb("AUGQ", [128, 8, 98], BF16)
    AUGK = sb("AUGK", [128, 8, 98], BF16)
    GLU = sb("GLU", [128, 4, 30 + T], F32)
    PS_ = sb("PSB", [128, NSUB, 256], F32)
    arena0 = off[0]
    QAT = sb("QAT", [128, 5, T], BF16)
    QT = sb("QT", [98, 8, T], BF16)
    CT = sb("CT", [128, 4, T], BF16)
    PT = [sb("PT%d" % i, [128, 1024], BF16) for i in range(2)]
    ATT = sb("ATT", [128, 4, T], BF16)
    OS = [sb("OS%d" % i, [64, T], F32) for i in range(2)]
    RDEN = sb("RDEN", [65, T], F32)
    RHL = sb("RHL", [128, 2, T], BF16)
    sub0 = off[0]
    QKA = sb("QKA", [128, NSUB, 640], BF16)
    KR = sb("KR", [128, NSUB, 32], F32)
    QR = sb("QR", [128, 8, 32], F32)
    RA = sb("RA", [128, 8, 32], F32)
    RB = sb("RB", [128, 8, 32], F32)
    SQ = sb("SQ", [128, 768], F32)
    VS = sb("VS", [128, 8, 64], F32)
    NRM = sb("NRM", [128, 32], F32)
    sub1 = off[0]
    off[0] = sub0
    SIG = sb("SIG", [128, 2, T], F32)
    ACC = sb("ACC", [128, 4, T], F32)
    CBF = sb("CBF", [128, 4, T], BF16)
    CSQ = sb("CSQ", [128, 4, T], BF16)
    LNM = sb("LNM", [128, T], F32)
    LNR = sb("LNR", [128, T], F32)
    LNT = sb("LNT", [128, 2, T], F32)
    off[0] = max(off[0], sub1)
    arena1 = off[0]
    off[0] = arena0
    HID = sb("HID", [128, NFC, T], BF16)
    GS = [sb("GS%d" % i, [128, 2 + T], F32) for i in range(3)]
    A1 = [sb("A1_%d" % i, [128, T], F32) for i in range(3)]
    A2 = [sb("A2_%d" % i, [128, T], F32) for i in range(3)]
    PBF = sb("PBF", [128, NSUB, 256], BF16)
    PTT = sb("PTT", [128, 2, T], BF16)
    E = sb("E", [128, 1024], F32)
    SG = sb("SG", [128, 1024], F32)
    arena2 = off[0]
    off[0] = max(arena1, arena2)
    sbuf_used = off[0]

    PSF = nc.alloc_psum_tensor("PS", [128, 8, 512], F32).ap()
    PSB16 = PSF.bitcast(BF16)

    R = {}
    for n, ns in [("KT", 32), ("VP", 32), ("GROW", 1), ("VEC", 1), ("CST", 1), ("CONSTB", 1), ("TAB", 1),
                  ("POSF", 1), ("GH", NFC), ("SM", 16), ("H", NSUB), ("XNT", 8), ("XB", NSUB), ("JUNK", 1),
                  ("TMPF", 1), ("ARENA", 1), ("QKA", NSUB), ("QAT", 1), ("KR", NSUB), ("QR", 1), ("RA", 1),
                  ("RB", 1), ("SQ", 1), ("VS", 1), ("NRM", 1), ("AUGQ", 1), ("AUGK", 1), ("QT", 1),
                  ("GLU", 4), ("SIG", 2), ("ACC", 4), ("CBF", 4), ("CSQ", 4), ("CT", 4), ("LN", 1),
                  ("PT", 3), ("ATT", 8), ("OS", 2), ("RDEN", 1), ("HID", NFC), ("GS", 3), ("GSH", 3), ("A1", 3), ("A2", 3),
                  ("PSB", 1), ("PBF", 1), ("PTT", 1), ("E", 1), ("SG", 1), ("PSUM", 8), ("RING", NSLOT),
                  ("X", 1), ("Y", 1), ("WS", NPIECE), ("DBG", 1)]:
        R[n] = Res(n, ns)
    R["PSUM"].exclusive = True
    DKEY = {}

    def dkey(n):
        if n not in DKEY:
            DKEY[n] = Res("k" + n)
        return DKEY[n]

    MIXER_RES = ["QKA", "QAT", "KR", "QR", "RA", "RB", "SQ", "VS", "NRM", "QT", "SIG",
                 "ACC", "CBF", "CSQ", "CT", "LN", "PT", "ATT", "OS", "RDEN"]
    S3_RES = ["QKA", "KR", "QR", "RA", "RB", "SQ", "VS", "NRM"]
    S4_RES = ["SIG", "ACC", "CBF", "CSQ", "LN"]
    FFN_RES = ["HID", "GS", "GSH", "A1", "A2", "PBF", "PTT", "E", "SG"]

    def bank(b):
        return R["PSUM"][b]

    def bc(ap, dims):
        return AP(ap.tensor, ap.offset, [list(ap.ap[0])] + [list(d) for d in dims])

    S.dma("sp", lambda e: e.dma_start(out=CST, in_=const_d), writes=[R["CST"]], key=dkey("cst"))
    S.dma("sp", lambda e: e.dma_start(out=VEC, in_=vec_d), writes=[R["VEC"]], key=dkey("vec"))
    S.dma("sp", lambda e: e.dma_start(out=GROW, in_=grow_d), writes=[R["GROW"]], key=dkey("grow"))
    S.dve(lambda e: e.tensor_copy(out=IDB, in_=CST[:, 0:128]), reads=[R["CST"]], writes=[R["CONSTB"]])
    S.dve(lambda e: e.tensor_copy(out=MASK, in_=CST[:, 128:256]), reads=[R["CST"]], writes=[R["CONSTB"]])
    S.dve(lambda e: e.memset(ONESB, 1.0), writes=[R["CONSTB"]])
    S.dve(lambda e: e.memset(ONESF, 1.0), writes=[R["CONSTB"]])
    S.dve(lambda e: e.memset(SEL, 0.0), writes=[R["CONSTB"]])
    S.dve(lambda e: e.memset(SEL[64:65, :], 1.0), writes=[R["CONSTB"]])
    S.dve(lambda e: e.memset(AUGQ, 1.0), writes=[R["AUGQ"]])
    S.dve(lambda e: e.memset(AUGK, 1.0), writes=[R["AUGK"]])
    S.dve(lambda e: e.memset(GH, 0.0), writes=[R["GH"]])
    S.dve(lambda e: e.memset(SM[:, 48:49], EPS), writes=[R["SM"][14]])
    S.dve(lambda e: e.memset(GLU, 0.0), writes=[R["GLU"]])
    POSI = bc(TMPF, [[1, 128]]).bitcast(I32)
    S.dma("sp", lambda e: e.dma_start(out=POSI[0:32, :], in_=pos_d), writes=[R["TMPF"]], key=dkey("pos"))
    S.dve(lambda e: e.tensor_copy(out=TMPF[0:32, 128:256], in_=POSI[0:32, :]), reads=[R["TMPF"]], writes=[R["TMPF"]])
    S.pe(lambda e: e.transpose(out=PSF[:, 0, 0:32], in_=TMPF[0:32, 128:256], identity=CST[0:32, 0:32]),
         reads=[R["TMPF"], R["CST"]], writes=[bank(0)])
    S.dve(lambda e: e.tensor_copy(out=POSF, in_=PSF[:, 0, 0:32]), reads=[bank(0)], writes=[R["POSF"]])
    TT = TMPF[:, 0:512]
    TK = TMPF[:, 512:1024]
    TKI = TK.bitcast(I32)
    INV2PI = float(1.0 / (2.0 * np.pi))

    def table(dst, shift, negate):
        S.dve(lambda e: e.tensor_tensor(out=bc(TT, [[16, 32], [1, 16]]), in0=bc(POSF, [[1, 32], [0, 16]]),
                                        in1=bc(CST[:, 256:272], [[0, 32], [1, 16]]), op=ALU.mult),
              reads=[R["POSF"], R["CST"]], writes=[R["TMPF"]])
        S.dve(lambda e: e.tensor_scalar(out=TT, in0=TT, scalar1=INV2PI, scalar2=shift, op0=ALU.mult, op1=ALU.add),
              reads=[R["TMPF"]], writes=[R["TMPF"]])
        S.dve(lambda e: e.tensor_copy(out=TKI, in_=TT), reads=[R["TMPF"]], writes=[R["TMPF"]])
        S.dve(lambda e: e.tensor_copy(out=TK, in_=TKI), reads=[R["TMPF"]], writes=[R["TMPF"]])
        S.dve(lambda e: e.tensor_tensor(out=TT, in0=TT, in1=TK, op=ALU.subtract), reads=[R["TMPF"]], writes=[R["TMPF"]])
        S.dve(lambda e: e.tensor_scalar(out=TT, in0=TT, scalar1=0.49999, scalar2=-0.49999, op0=ALU.min, op1=ALU.max),
              reads=[R["TMPF"]], writes=[R["TMPF"]])
        sc = float(-2.0 * np.pi) if negate else float(2.0 * np.pi)
        S.act(lambda e: e.activation(out=dst.rearrange("p a b -> p (a b)"), in_=TT, func=AF.Sin, scale=sc),
              reads=[R["TMPF"]], writes=[R["TAB"]])

    table(SIN, 0.0, False)
    table(NSIN, 0.0, True)
    table(COS, 0.25, False)

    def conv_dma(out_ap, in_ap, pieces, name):
        S.dma("pool", lambda e: e.dma_start(out=out_ap, in_=in_ap), writes=[R["WS"][i] for i in pieces],
              key=dkey("cv" + name), name=name)

    conv_dma(ws_d[P_WINT:P_WINT + 8, :, 0:672], w_in_d[:, 0:672].rearrange("(kc p) n -> kc p n", p=128),
             range(P_WINT, P_WINT + 8), "wint")
    conv_dma(ws_d[P_WQB:P_WQB + 3, :, 0:768], w_qb_d.rearrange("(kc p) n -> kc p n", p=128), range(P_WQB, P_WQB + 3), "wqb")
    conv_dma(ws_d[P_WKVB:P_WKVB + 2, :, :], w_kvb_d.rearrange("(kc p) n -> kc p n", p=128), range(P_WKVB, P_WKVB + 2), "wkvb")
    for cg in range(4):
        for g in range(2):
            col = 672 + g * 512 + cg * 128
            conv_dma(ws_d[P_WINF + 2 * cg + g].rearrange("p (kc m) -> p kc m", kc=8),
                     w_in_d[:, col:col + 128].rearrange("(kc p) m -> p kc m", p=128), [P_WINF + 2 * cg + g], "winf%d%d" % (cg, g))
    conv_dma(ws_d[P_WO:P_WO + 8], w_o_d.rearrange("(kc p) n -> kc p n", p=128), range(P_WO, P_WO + 8), "wo")
    for c in range(NFC):
        conv_dma(ws_d[P_WGU + 2 * c].rearrange("p (kc m) -> p kc m", kc=8),
                 w_g_d[:, c * 128:(c + 1) * 128].rearrange("(kc p) m -> p kc m", p=128), [P_WGU + 2 * c], "wg%d" % c)
        conv_dma(ws_d[P_WGU + 2 * c + 1].rearrange("p (kc m) -> p kc m", kc=8),
                 w_u_d[:, c * 128:(c + 1) * 128].rearrange("(kc p) m -> p kc m", p=128), [P_WGU + 2 * c + 1], "wu%d" % c)
    for half in range(2):
        conv_dma(ws_d[P_WD + 11 * half:P_WD + 11 * (half + 1)],
                 w_d_d[half * 1408:(half + 1) * 1408, :].rearrange("(c p) n -> c p n", p=128),
                 range(P_WD + 11 * half, P_WD + 11 * (half + 1)), "wd%d" % half)
    conv_dma(ws_d[P_WPP:P_WPP + 2], w_pp_d.rearrange("(kc p) n -> kc p n", p=128), range(P_WPP, P_WPP + 2), "wpp")
    conv_dma(ws_d[P_WPG:P_WPG + 8], w_pg_d.rearrange("(kc p) n -> kc p n", p=128), range(P_WPG, P_WPG + 8), "wpg")

    ring_state = {"issued": 0}

    def piece(tile_i, idx):
        g = tile_i * NPIECE + idx
        upto = min(g + NSLOT - 2, ntiles * NPIECE - 1)
        while ring_state["issued"] <= upto:
            gi = ring_state["issued"]
            sl = gi % NSLOT
            pi = gi % NPIECE
            ncol = 672 if pi < P_WINT + 8 else (768 if pi < P_WQB + 3 else 1024)
            S.dma("sp", lambda e, sl=sl, pi=pi, ncol=ncol: e.dma_start(out=RING[sl][:, 0:ncol], in_=ws_d[pi, :, 0:ncol]),
                  reads=[R["WS"][pi]], writes=[R["RING"][sl]], key=dkey("ring%d" % sl), name="ld%d" % pi)
            ring_state["issued"] += 1
        sl = g % NSLOT
        return RING[sl], R["RING"][sl]

    def rstd_from_ss(ss_ap, out_ap, n, reads, writes):
        S.act(lambda e: e.activation(out=out_ap, in_=ss_ap, func=AF.Ln, scale=1.0 / n, bias=SM[:, 48:49]),
              reads=reads + [R["SM"][14]], writes=writes)
        S.act(lambda e: e.activation(out=out_ap, in_=out_ap, func=AF.Exp, scale=-0.5), reads=writes, writes=writes)

    def norm_transpose(gcol, tb):
        for j in range(NSUB):
            if gcol is not None:
                S.act(lambda e, j=j: e.activation(out=JUNK, in_=H[:, j, :], func=AF.Square, accum_out=SM[:, j:j + 1]),
                      reads=[R["H"][j]], writes=[R["JUNK"], R["SM"][0]])
        if gcol is not None:
            rstd_from_ss(SM[:, 0:NSUB], SM[:, 2:2 + NSUB], 1024.0, [R["SM"][0]], [R["SM"][1]])
        for j in range(NSUB):
            if gcol is not None:
                S.dve(lambda e, j=j: e.tensor_scalar(out=XB[:, j, :], in0=H[:, j, :], scalar1=SM[:, 2 + j:3 + j], scalar2=None,
                                                     op0=ALU.mult), reads=[R["H"][j], R["SM"][1]], writes=[R["XB"][j]])
            else:
                S.dve(lambda e, j=j: e.tensor_copy(out=XB[:, j, :], in_=H[:, j, :]), reads=[R["H"][j]], writes=[R["XB"][j]])
            b = tb[j]
            for c in range(8):
                S.pe(lambda e, j=j, c=c, b=b: e.transpose(out=PSB16[:, b, c * 128:(c + 1) * 128], in_=XB[:, j, c * 128:(c + 1) * 128],
                                                          identity=IDB), reads=[R["XB"][j], R["CONSTB"]], writes=[bank(b)])
            if gcol is not None:
                S.dve(lambda e, j=j, b=b: e.tensor_tensor(out=XNT[:, :, j * 128:(j + 1) * 128],
                                                          in0=PSB16[:, b, :].rearrange("p (c t) -> p c t", c=8),
                                                          in1=bc(VEC[:, gcol:gcol + 8], [[1, 8], [0, 128]]), op=ALU.mult),
                      reads=[bank(b), R["VEC"]], writes=[R["XNT"][c] for c in range(8)])
            else:
                S.act(lambda e, j=j, b=b: e.activation(out=XNT[:, :, j * 128:(j + 1) * 128],
                                                       in_=PSB16[:, b, :].rearrange("p (c t) -> p c t", c=8), func=AF.Copy),
                      reads=[bank(b)], writes=[R["XNT"][c] for c in range(8)])

    def post_norm_residual(banks2, j, grow_i):
        b0, b1 = banks2
        src = PSF[:, b0:b0 + 2, :].rearrange("p b n -> p (b n)")
        S.act(lambda e: e.activation(out=JUNK, in_=src, func=AF.Square, accum_out=SM[:, 4 + j:5 + j]),
              reads=[bank(b0), bank(b1)], writes=[R["JUNK"], R["SM"][2 + j]])
        rstd_from_ss(SM[:, 4 + j:5 + j], SM[:, 6 + j:7 + j], 1024.0, [R["SM"][2 + j]], [R["SM"][4 + j]])
        S.dve(lambda e: e.scalar_tensor_tensor(out=TMPF, in0=src, scalar=SM[:, 6 + j:7 + j], in1=GROW[:, grow_i, :],
                                               op0=ALU.mult, op1=ALU.mult),
              reads=[bank(b0), bank(b1), R["SM"][4 + j], R["GROW"]], writes=[R["TMPF"]])
        S.dve(lambda e: e.tensor_tensor(out=H[:, j, :], in0=H[:, j, :], in1=TMPF, op=ALU.add),
              reads=[R["H"][j], R["TMPF"]], writes=[R["H"][j]])

    def dump(name, ap, reads):
        if name in dbg_d:
            S.dma("pool", lambda e: e.dma_start(out=dbg_d[name], in_=ap), reads=reads, writes=[R["DBG"]], key=dkey("dbg" + name))

    final_ops = []

    class _Stop(Exception):
        pass

    def chk(name):
        S.cur = name + "+"
        if stop == name:
            raise _Stop()
    try:
      chk("pro")
      for it in range(ntiles):
          t0 = it * T
          chk('s0')
          for j in range(NSUB):
              S.dma("sp", lambda e, t0=t0, j=j: e.dma_start(out=H[:, j, :], in_=x_d[t0 + j * 128:t0 + (j + 1) * 128, :]),
                    reads=[R["X"]], writes=[R["H"][j]], key=dkey("xload%d" % j))
          S.dma("sp", lambda e, t0=t0: e.dma_start(out=PS_, in_=p_d[t0:t0 + T, :].rearrange("(j p) d -> p j d", p=128)),
                reads=[R["X"]], writes=[R["PSB"]], key=dkey("pload"))
          for n in MIXER_RES:
              pass
          norm_transpose(V_GPRE, (0, 1))
          if it == ntiles - 1:
              dump("xnt", XNT, [R["XNT"]])
          chk('s1')
          for kc in range(8):
              w, wr = piece(it, P_WINT + kc)
              for j in range(NSUB):
                  bA, bB = 2 + 2 * j, 3 + 2 * j
                  S.pe(lambda e, w=w, j=j, kc=kc, bA=bA: e.matmul(PSF[:, bA, :], lhsT=XNT[:, kc, j * 128:(j + 1) * 128], rhs=w[:, 0:512],
                                                                 start=(kc == 0), stop=(kc == 7)),
                       reads=[R["XNT"][kc], wr], writes=[bank(bA)])
                  S.pe(lambda e, w=w, j=j, kc=kc, bB=bB: e.matmul(PSF[:, bB, 0:160], lhsT=XNT[:, kc, j * 128:(j + 1) * 128], rhs=w[:, 512:672],
                                                                 start=(kc == 0), stop=(kc == 7)),
                       reads=[R["XNT"][kc], wr], writes=[bank(bB)])
          chk('s1b')
          for j in range(NSUB):
              bA, bB = 2 + 2 * j, 3 + 2 * j
              S.act(lambda e, bA=bA, j=j: e.activation(out=JUNK[:, 0:384], in_=PSF[:, bA, 0:384], func=AF.Square,
                                                       accum_out=SM[:, 8 + j:9 + j]), reads=[bank(bA)], writes=[R["JUNK"], R["SM"][6]])
              S.act(lambda e, bA=bA, j=j: e.activation(out=JUNK[:, 0:128], in_=PSF[:, bA, 384:512], func=AF.Square,
                                                       accum_out=SM[:, 10 + j:11 + j]), reads=[bank(bA)], writes=[R["JUNK"], R["SM"][6]])
              S.act(lambda e, bB=bB, j=j: e.activation(out=JUNK[:, 0:128], in_=PSF[:, bB, 0:128], func=AF.Square,
                                                       accum_out=SM[:, 12 + j:13 + j]), reads=[bank(bB)], writes=[R["JUNK"], R["SM"][6]])
              S.dve(lambda e, bA=bA, j=j: e.tensor_copy(out=QKA[:, j, 0:512], in_=PSF[:, bA, :]), reads=[bank(bA)], writes=[R["QKA"][j]])
              S.act(lambda e, bB=bB, j=j: e.activation(out=QKA[:, j, 512:640], in_=PSF[:, bB, 0:128], func=AF.Copy),
                    reads=[bank(bB)], writes=[R["QKA"][j]])
              S.dve(lambda e, bB=bB, j=j: e.tensor_copy(out=KR[:, j, :], in_=PSF[:, bB, 128:160]), reads=[bank(bB)], writes=[R["KR"][j]])
          chk('s1c')
          S.dve(lambda e: e.tensor_tensor(out=SM[:, 10:12], in0=SM[:, 10:12], in1=SM[:, 12:14], op=ALU.add),
                reads=[R["SM"][6]], writes=[R["SM"][6]])
          rstd_from_ss(SM[:, 8:10], SM[:, 16:18], 384.0, [R["SM"][6]], [R["SM"][7]])
          rstd_from_ss(SM[:, 10:12], SM[:, 18:20], 256.0, [R["SM"][6]], [R["SM"][8]])
          for j in range(NSUB):
              b = j
              for c in range(5):
                  S.pe(lambda e, j=j, c=c, b=b: e.transpose(out=PSB16[:, b, c * 128:(c + 1) * 128], in_=QKA[:, j, c * 128:(c + 1) * 128],
                                                            identity=IDB), reads=[R["QKA"][j], R["CONSTB"]], writes=[bank(b)])
              S.dve(lambda e, j=j, b=b: e.tensor_tensor(out=QAT[:, :, j * 128:(j + 1) * 128],
                                                        in0=PSB16[:, b, 0:640].rearrange("p (c t) -> p c t", c=5),
                                                        in1=bc(VEC[:, V_GQA:V_GQA + 5], [[1, 5], [0, 128]]), op=ALU.mult),
                    reads=[bank(b), R["VEC"]], writes=[R["QAT"]])
          if it == ntiles - 1:
              dump("qka", QKA, [R["QKA"]])
              dump("qat", QAT, [R["QAT"]])
              dump("kr", KR, [R["KR"]])
              dump("sm", SM, [R["SM"]])
          chk('s2')
          KVB = (6, 0)
          for kc in range(3):
              w, wr = piece(it, P_WQB + kc)
              for j in range(NSUB):
                  bq = 2 + 2 * j
                  S.pe(lambda e, w=w, j=j, kc=kc, bq=bq: e.matmul(PSF[:, bq, :], lhsT=QAT[:, kc, j * 128:(j + 1) * 128], rhs=w[:, 0:512],
                                                                 start=(kc == 0), stop=(kc == 2)), reads=[R["QAT"], wr], writes=[bank(bq)])
                  S.pe(lambda e, w=w, j=j, kc=kc, bq=bq: e.matmul(PSF[:, bq + 1, 0:256], lhsT=QAT[:, kc, j * 128:(j + 1) * 128], rhs=w[:, 512:768],
                                                                 start=(kc == 0), stop=(kc == 2)), reads=[R["QAT"], wr], writes=[bank(bq + 1)])
          for kc in range(2):
              w, wr = piece(it, P_WKVB + kc)
              for j in range(NSUB):
                  for hb in range(2):
                      S.pe(lambda e, w=w, j=j, kc=kc, hb=hb: e.matmul(PSF[:, KVB[j] + hb, :], lhsT=QAT[:, 3 + kc, j * 128:(j + 1) * 128],
                                                                     rhs=w[:, hb * 512:(hb + 1) * 512], start=(kc == 0), stop=(kc == 1)),
                           reads=[R["QAT"], wr], writes=[bank(KVB[j] + hb)])
          for j in range(NSUB):
              bq = 2 + 2 * j
              kb = KVB[j]
              blk = it * NSUB + j
              rq = SM[:, 16 + j:17 + j]
              rkv = SM[:, 18 + j:19 + j]
              qps = PSF[:, bq:bq + 2, :].rearrange("p b n -> p (b n)")[:, 0:768].rearrange("p (h d) -> p h d", h=8)
              kvps = PSF[:, kb:kb + 2, :].rearrange("p b n -> p (b n)").rearrange("p (h d) -> p h d", h=8)
              qrd = [bank(bq), bank(bq + 1)]
              kvrd = [bank(kb), bank(kb + 1)]
              cosb = bc(COS[:, blk, :], [[0, 8], [0, 2], [1, 16]])
              sinb = bc(SIN[:, blk, :], [[0, 8], [1, 16]])
              nsinb = bc(NSIN[:, blk, :], [[0, 8], [1, 16]])
              S.act(lambda e, qps=qps, rq=rq: e.activation(out=AUGQ[:, :, 0:64], in_=qps[:, :, 0:64], func=AF.Copy, scale=rq),
                    reads=qrd + [R["SM"][7]], writes=[R["AUGQ"]])
              S.dve(lambda e, qps=qps, rq=rq: e.tensor_scalar(out=QR, in0=qps[:, :, 64:96], scalar1=rq, scalar2=None, op0=ALU.mult),
                    reads=qrd + [R["SM"][7]], writes=[R["QR"]])
              S.act(lambda e, bq=bq, rq=rq: e.activation(out=SQ, in_=PSF[:, bq:bq + 2, :].rearrange("p b n -> p (b n)")[:, 0:768],
                                                         func=AF.Square, scale=rq), reads=qrd + [R["SM"][7]], writes=[R["SQ"]])
              S.dve(lambda e: e.tensor_reduce(out=NRM[:, 0:8], in_=SQ.rearrange("p (h d) -> p h d", h=8), axis=AX.X, op=ALU.add),
                    reads=[R["SQ"]], writes=[R["NRM"]])
              S.dve(lambda e: e.tensor_scalar(out=AUGQ[:, :, 96:97], in0=NRM[:, 0:8].unsqueeze(2), scalar1=-0.5, scalar2=None, op0=ALU.mult),
                    reads=[R["NRM"]], writes=[R["AUGQ"]])
              S.dve(lambda e, cosb=cosb: e.tensor_tensor(out=RA.rearrange("p h (a f) -> p h a f", a=2), in0=QR.rearrange("p h (a f) -> p h a f", a=2),
                                                         in1=cosb, op=ALU.mult), reads=[R["QR"], R["TAB"]], writes=[R["RA"]])
              S.dve(lambda e, nsinb=nsinb: e.tensor_tensor(out=RB[:, :, 0:16], in0=QR[:, :, 16:32], in1=nsinb, op=ALU.mult),
                    reads=[R["QR"], R["TAB"]], writes=[R["RB"]])
              S.dve(lambda e, sinb=sinb: e.tensor_tensor(out=RB[:, :, 16:32], in0=QR[:, :, 0:16], in1=sinb, op=ALU.mult),
                    reads=[R["QR"], R["TAB"]], writes=[R["RB"]])
              S.dve(lambda e: e.tensor_tensor(out=AUGQ[:, :, 64:96], in0=RA, in1=RB, op=ALU.add),
                    reads=[R["RA"], R["RB"]], writes=[R["AUGQ"]])
              S.act(lambda e, kvps=kvps, rkv=rkv: e.activation(out=AUGK[:, :, 0:64], in_=kvps[:, :, 0:64], func=AF.Copy, scale=rkv),
                    reads=kvrd + [R["SM"][8]], writes=[R["AUGK"]])
              S.act(lambda e, kvps=kvps, rkv=rkv: e.activation(out=VS, in_=kvps[:, :, 0:64], func=AF.Square, scale=rkv),
                    reads=kvrd + [R["SM"][8]], writes=[R["VS"]])
              S.dve(lambda e: e.tensor_reduce(out=NRM[:, 8:16], in_=VS, axis=AX.X, op=ALU.add), reads=[R["VS"]], writes=[R["NRM"]])
              S.act(lambda e, j=j: e.activation(out=JUNK[:, 0:32], in_=KR[:, j, :], func=AF.Square, accum_out=NRM[:, 16:17]),
                    reads=[R["KR"][j]], writes=[R["JUNK"], R["NRM"]])
              cos1 = bc(COS[:, blk, :], [[0, 2], [1, 16]])
              S.dve(lambda e, j=j, cos1=cos1: e.tensor_tensor(out=QR[:, 0, :].rearrange("p (a f) -> p a f", a=2),
                                                              in0=KR[:, j, :].rearrange("p (a f) -> p a f", a=2), in1=cos1, op=ALU.mult),
                    reads=[R["KR"][j], R["TAB"], R["AUGQ"]], writes=[R["QR"]])
              S.dve(lambda e, j=j, blk=blk: e.tensor_tensor(out=RB[:, 0, 0:16], in0=KR[:, j, 16:32], in1=NSIN[:, blk, :], op=ALU.mult),
                    reads=[R["KR"][j], R["TAB"], R["AUGQ"]], writes=[R["RB"]])
              S.dve(lambda e, j=j, blk=blk: e.tensor_tensor(out=RB[:, 0, 16:32], in0=KR[:, j, 0:16], in1=SIN[:, blk, :], op=ALU.mult),
                    reads=[R["KR"][j], R["TAB"]], writes=[R["RB"]])
              S.dve(lambda e: e.tensor_tensor(out=RA[:, 0, :], in0=QR[:, 0, :], in1=RB[:, 0, :], op=ALU.add),
                    reads=[R["QR"], R["RB"]], writes=[R["RA"]])
              S.dve(lambda e: e.tensor_copy(out=AUGK[:, :, 64:96], in_=bc(RA[:, 0, :], [[0, 8], [1, 32]])),
                    reads=[R["RA"]], writes=[R["AUGK"]])
              S.dve(lambda e: e.tensor_scalar(out=NRM[:, 8:16], in0=NRM[:, 8:16], scalar1=NRM[:, 16:17], scalar2=-0.5, op0=ALU.add, op1=ALU.mult),
                    reads=[R["NRM"]], writes=[R["NRM"]])
              S.dve(lambda e: e.tensor_copy(out=AUGK[:, :, 97:98], in_=NRM[:, 8:16].unsqueeze(2)), reads=[R["NRM"]], writes=[R["AUGK"]])
              S.act(lambda e: e.activation(out=NRM[:, 24:32].unsqueeze(2), in_=AUGK[:, :, 97:98], func=AF.Exp, scale=-ATT_SCALE),
                    reads=[R["AUGK"]], writes=[R["NRM"]])
              S.act(lambda e, kvps=kvps, rkv=rkv: e.activation(out=VS, in_=kvps[:, :, 64:128], func=AF.Copy, scale=rkv),
                    reads=kvrd + [R["SM"][8], R["NRM"]], writes=[R["VS"]])
              S.dve(lambda e, blk=blk: e.tensor_tensor(out=VP[:, blk, :, 0:64], in0=VS, in1=bc(NRM[:, 24:32], [[1, 8], [0, 64]]), op=ALU.mult),
                    reads=[R["VS"], R["NRM"]], writes=[R["VP"][blk]])
              S.dve(lambda e, blk=blk: e.tensor_copy(out=VP[:, blk, :, 64:65], in_=NRM[:, 24:32].unsqueeze(2)),
                    reads=[R["NRM"]], writes=[R["VP"][blk]])
              for h in range(8):
                  S.pe(lambda e, h=h, bq=bq: e.transpose(out=PSB16[0:98, bq, h * 128:(h + 1) * 128], in_=AUGK[:, h, :], identity=IDB),
                       reads=[R["AUGK"], R["CONSTB"]], writes=[bank(bq)])
              for h in range(8):
                  S.pe(lambda e, h=h, bq=bq: e.transpose(out=PSB16[0:98, bq + 1, h * 128:(h + 1) * 128], in_=AUGQ[:, h, :], identity=IDB),
                       reads=[R["AUGQ"], R["CONSTB"]], writes=[bank(bq + 1)])
              S.act(lambda e, blk=blk, bq=bq: e.activation(out=KT[:, :, blk * 128:(blk + 1) * 128],
                                                           in_=PSB16[0:98, bq, :].rearrange("p (h t) -> p h t", h=8), func=AF.Copy),
                    reads=[bank(bq)], writes=[R["KT"][blk]])
              S.dve(lambda e, j=j, bq=bq: e.tensor_copy(out=QT[:, :, j * 128:(j + 1) * 128], in_=PSB16[0:98, bq + 1, :].rearrange("p (h t) -> p h t", h=8)),
                    reads=[bank(bq + 1)], writes=[R["QT"]])
          if it == ntiles - 1:
              dump("qt", QT, [R["QT"]])
              dump("kt", KT[:, :, t0:t0 + T], [R["KT"]])
              dump("vp", VP[:, it * NSUB:(it + 1) * NSUB], [R["VP"]])
          chk('s3')
          S.dve(lambda e: e.memset(SM[:, 62:63], 0.0), writes=[R[n] for n in S3_RES + S4_RES] + [R["SM"][15]])
          for cg in range(4):
              wa, war = piece(it, P_WINF + 2 * cg)
              wg_, wgr = piece(it, P_WINF + 2 * cg + 1)
              ba, bg = 2 + 2 * (cg % 2), 3 + 2 * (cg % 2)
              for kc in range(8):
                  S.pe(lambda e, wa=wa, kc=kc, ba=ba: e.matmul(PSF[:, ba, 0:T], lhsT=wa[:, kc * 128:(kc + 1) * 128], rhs=XNT[:, kc, :],
                                                              start=(kc == 0), stop=(kc == 7)), reads=[R["XNT"][kc], war], writes=[bank(ba)])
              for kc in range(8):
                  S.pe(lambda e, wg_=wg_, kc=kc, bg=bg: e.matmul(PSF[:, bg, 0:T], lhsT=wg_[:, kc * 128:(kc + 1) * 128], rhs=XNT[:, kc, :],
                                                                start=(kc == 0), stop=(kc == 7)), reads=[R["XNT"][kc], wgr], writes=[bank(bg)])
              S.act(lambda e, cg=cg, bg=bg: e.activation(out=SIG[:, cg % 2, :], in_=PSF[:, bg, 0:T], func=AF.Sigmoid),
                    reads=[bank(bg)], writes=[R["SIG"][cg % 2]])
              S.dve(lambda e, cg=cg, ba=ba: e.tensor_tensor(out=GLU[:, cg, 30:30 + T], in0=PSF[:, ba, 0:T], in1=SIG[:, cg % 2, :], op=ALU.mult),
                    reads=[bank(ba), R["SIG"][cg % 2]], writes=[R["GLU"][cg]])
          conv_pending = []
          for jt in range(31):
              for cg in range(4):
                  wcol = VEC[:, V_CW + cg * 31 + jt:V_CW + cg * 31 + jt + 1]
                  if jt == 0:
                      conv_pending.append(lambda cg=cg, jt=jt, wcol=wcol: S.dve(
                          lambda e: e.tensor_scalar(out=ACC[:, cg, :], in0=GLU[:, cg, jt:jt + T], scalar1=wcol, scalar2=None, op0=ALU.mult),
                          reads=[R["GLU"][cg], R["VEC"]], writes=[R["ACC"][cg]], name="chain"))
                  else:
                      conv_pending.append(lambda cg=cg, jt=jt, wcol=wcol: S.dve(
                          lambda e: e.scalar_tensor_tensor(out=ACC[:, cg, :], in0=GLU[:, cg, jt:jt + T], scalar=wcol, in1=ACC[:, cg, :],
                                                           op0=ALU.mult, op1=ALU.add),
                          reads=[R["GLU"][cg], R["VEC"], R["ACC"][cg]], writes=[R["ACC"][cg]], name="chain"))
          conv_pending.reverse()

          def conv_some(n):
              for _ in range(n):
                  if conv_pending:
                      conv_pending.pop()()
          chk('s4')
          ln_state = [False]
          def emit_ln():
              for cg in range(4):
                  S.pool(lambda e, cg=cg: e.tensor_copy(out=GLU[:, cg, 0:30], in_=GLU[:, cg, T:T + 30]), reads=[R["GLU"][cg]], writes=[R["GLU"][cg]])
                  S.act(lambda e, cg=cg: e.activation(out=CBF[:, cg, :], in_=ACC[:, cg, :], func=AF.Identity, bias=VEC[:, V_CB + cg:V_CB + cg + 1]),
                        reads=[R["ACC"][cg], R["VEC"]], writes=[R["CBF"][cg]])
                  S.act(lambda e, cg=cg: e.activation(out=CSQ[:, cg, :], in_=ACC[:, cg, :], func=AF.Square, bias=VEC[:, V_CB + cg:V_CB + cg + 1]),
                        reads=[R["ACC"][cg], R["VEC"]], writes=[R["CSQ"][cg]])
              for cg in range(4):
                  S.pe(lambda e, cg=cg: e.matmul(PSF[:, 7, 0:T], lhsT=ONESB, rhs=CBF[:, cg, :], start=(cg == 0), stop=(cg == 3)),
                       reads=[R["CBF"][cg], R["CONSTB"]], writes=[bank(7)])
              for cg in range(4):
                  S.pe(lambda e, cg=cg: e.matmul(PSF[:, 6, 0:T], lhsT=ONESB, rhs=CSQ[:, cg, :], start=(cg == 0), stop=(cg == 3)),
                       reads=[R["CSQ"][cg], R["CONSTB"]], writes=[bank(6)])
              S.dve(lambda e: e.tensor_scalar(out=LNM, in0=PSF[:, 7, 0:T], scalar1=1.0 / 512, scalar2=None, op0=ALU.mult), reads=[bank(7)], writes=[R["LN"]])
              S.dve(lambda e: e.tensor_tensor(out=LNT[:, 0, :], in0=LNM, in1=LNM, op=ALU.mult), reads=[R["LN"]], writes=[R["LN"]])
              S.dve(lambda e: e.scalar_tensor_tensor(out=LNR, in0=PSF[:, 6, 0:T], scalar=1.0 / 512, in1=LNT[:, 0, :], op0=ALU.mult, op1=ALU.subtract),
                    reads=[bank(6), R["LN"]], writes=[R["LN"]])
              S.act(lambda e: e.activation(out=LNR, in_=LNR, func=AF.Ln, bias=SM[:, 48:49]), reads=[R["LN"], R["SM"][14]], writes=[R["LN"]])
              S.act(lambda e: e.activation(out=LNR, in_=LNR, func=AF.Exp, scale=-0.5), reads=[R["LN"]], writes=[R["LN"]])
              for cg in range(4):
                  S.dve(lambda e, cg=cg: e.scalar_tensor_tensor(out=LNT[:, cg % 2, :], in0=ACC[:, cg, :], scalar=VEC[:, V_CB + cg:V_CB + cg + 1], in1=LNM,
                                                                op0=ALU.add, op1=ALU.subtract),
                        reads=[R["ACC"][cg], R["LN"], R["VEC"]], writes=[R["LN"]])
                  S.dve(lambda e, cg=cg: e.tensor_tensor(out=LNT[:, cg % 2, :], in0=LNT[:, cg % 2, :], in1=LNR, op=ALU.mult), reads=[R["LN"]], writes=[R["LN"]])
                  S.act(lambda e, cg=cg: e.activation(out=CT[:, cg, :], in_=LNT[:, cg % 2, :], func=AF.Silu, scale=VEC[:, V_LNG + cg:V_LNG + cg + 1],
                                                      bias=VEC[:, V_LNB + cg:V_LNB + cg + 1]), reads=[R["LN"], R["VEC"]], writes=[R["CT"][cg]])
          conv_some(28)
          SSETS = (2, 0)
          units = []
          for h in range(8):
              kp = 0
              while kp + 1 < it:
                  units.append((h, [2 * kp, 2 * kp + 1, 2 * kp + 2, 2 * kp + 3], False, kp == 0))
                  kp += 2
              if kp < it:
                  units.append((h, [2 * kp, 2 * kp + 1], False, kp == 0))
              units.append((h, [2 * it, 2 * it + 1], True, it == 0))
          nun = len(units)

          def emit_qk(u):
              h, kts, band, first = units[u]
              sb0 = SSETS[u % 2]
              if not band:
                  for a_, kt in enumerate(kts):
                      sb_ = sb0 + a_ // 2
                      S.pe(lambda e, h=h, kt=kt, a_=a_, sb_=sb_: e.matmul(PSF[:, sb_, (a_ % 2) * T:(a_ % 2 + 1) * T], lhsT=KT[:, h, kt * 128:(kt + 1) * 128],
                                                                         rhs=QT[:, h, :], start=True, stop=True),
                           reads=[R["KT"][kt], R["QT"]], writes=[bank(sb_)])
              else:
                  sb_ = sb0
                  k0, k1 = kts
                  S.pe(lambda e, h=h, sb_=sb_, k0=k0: e.matmul(PSF[:, sb_, 0:T], lhsT=KT[:, h, k0 * 128:(k0 + 1) * 128], rhs=QT[:, h, :], start=True, stop=False),
                       reads=[R["KT"][k0], R["QT"]], writes=[bank(sb_)])
                  S.pe(lambda e, sb_=sb_: e.matmul(PSF[:, sb_, 0:128], lhsT=IDB, rhs=MASK, start=False, stop=True),
                       reads=[R["CONSTB"]], writes=[bank(sb_)])
                  S.pe(lambda e, h=h, sb_=sb_, k1=k1: e.matmul(PSF[:, sb_, 384:512], lhsT=KT[:, h, k1 * 128:(k1 + 1) * 128], rhs=QT[:, h, 128:256], start=True, stop=False),
                       reads=[R["KT"][k1], R["QT"]], writes=[bank(sb_)])
                  S.pe(lambda e, sb_=sb_: e.matmul(PSF[:, sb_, 384:512], lhsT=IDB, rhs=MASK, start=False, stop=True),
                       reads=[R["CONSTB"]], writes=[bank(sb_)])

          def emit_exp(u):
              h, kts, band, first = units[u]
              sb0 = SSETS[u % 2]
              pt, ptr = PT[u % 2], R["PT"][u % 2]
              if not band:
                  nb = len(kts) // 2
                  S.act(lambda e, sb0=sb0, pt=pt, nb=nb: e.activation(out=pt[:, 0:nb * 512], in_=PSF[:, sb0:sb0 + nb, :].rearrange("p b n -> p (b n)"),
                                                                      func=AF.Exp, scale=ATT_SCALE),
                        reads=[bank(sb0 + i) for i in range(nb)], writes=[ptr])
              else:
                  S.act(lambda e, sb0=sb0, pt=pt: e.activation(out=pt[:, 0:T], in_=PSF[:, sb0, 0:T], func=AF.Exp, scale=ATT_SCALE),
                        reads=[bank(sb0)], writes=[ptr])
                  S.act(lambda e, sb0=sb0, pt=pt: e.activation(out=pt[:, 384:512], in_=PSF[:, sb0, 384:512], func=AF.Exp, scale=ATT_SCALE),
                        reads=[bank(sb0)], writes=[ptr])

          def emit_pv(u):
              h, kts, band, first = units[u]
              ob = 4 + (h % 2)
              pt, ptr = PT[u % 2], R["PT"][u % 2]
              O_ps = PSF[0:65, ob, 0:T]
              if not band:
                  for a_, kt in enumerate(kts):
                      S.pe(lambda e, h=h, kt=kt, a_=a_, pt=pt, O_ps=O_ps, first=first: e.matmul(O_ps, lhsT=VP[:, kt, h, :], rhs=pt[:, a_ * T:(a_ + 1) * T],
                                                                                              start=(first and a_ == 0), stop=False),
                           reads=[R["VP"][kt], ptr], writes=[bank(ob)])
              else:
                  k0, k1 = kts
                  S.pe(lambda e, h=h, pt=pt, O_ps=O_ps, first=first, k0=k0: e.matmul(O_ps, lhsT=VP[:, k0, h, :], rhs=pt[:, 0:T], start=first, stop=False),
                       reads=[R["VP"][k0], ptr], writes=[bank(ob)])
                  S.pe(lambda e, h=h, pt=pt, ob=ob, k1=k1: e.matmul(PSF[0:65, ob, 128:256], lhsT=VP[:, k1, h, :], rhs=pt[:, 384:512], start=False, stop=True),
                       reads=[R["VP"][k1], ptr], writes=[bank(ob)])

          S.pool(lambda e: e.memset(RHL, 0.0), writes=[R["RDEN"]])

          def emit_norm_a(h):
              ob = 4 + (h % 2)
              S.act(lambda e, ob=ob: e.activation(out=RDEN[64:65, :], in_=PSF[64:65, ob, 0:T], func=AF.Ln), reads=[bank(ob)], writes=[R["RDEN"]])
              S.act(lambda e: e.activation(out=RDEN[64:65, :], in_=RDEN[64:65, :], func=AF.Exp, scale=-1.0), reads=[R["RDEN"]], writes=[R["RDEN"]])
              osb, osr = OS[h % 2], R["OS"][h % 2]
              S.act(lambda e, ob=ob, osb=osb: e.activation(out=osb, in_=PSF[0:64, ob, 0:T], func=AF.Copy), reads=[bank(ob)], writes=[osr])
              conv_some(8)
              S.dve(lambda e: e.tensor_copy(out=RHL[64:65, 0, :], in_=RDEN[64:65, :]), reads=[R["RDEN"]], writes=[R["RDEN"]])
              S.dve(lambda e: e.tensor_tensor(out=RHL[64:65, 1, :], in0=RDEN[64:65, :], in1=RHL[64:65, 0, :], op=ALU.subtract),
                    reads=[R["RDEN"]], writes=[R["RDEN"]])

          def emit_norm_b(h):
              osb, osr = OS[h % 2], R["OS"][h % 2]
              S.pe(lambda e: e.matmul(PSF[0:64, 6, 0:T], lhsT=SEL, rhs=RHL[:, 0, :], start=True, stop=False),
                   reads=[R["RDEN"], R["CONSTB"]], writes=[bank(6)])
              S.pe(lambda e: e.matmul(PSF[0:64, 6, 0:T], lhsT=SEL, rhs=RHL[:, 1, :], start=False, stop=True),
                   reads=[R["RDEN"], R["CONSTB"]], writes=[bank(6)])
              po = (h % 2) * 64
              conv_some(8)
              S.dve(lambda e, h=h, po=po, osb=osb: e.tensor_tensor(out=ATT[po:po + 64, h // 2, :], in0=osb, in1=PSF[0:64, 6, 0:T], op=ALU.mult),
                    reads=[osr, bank(6)], writes=[R["ATT"][h]])
              if not conv_pending and not ln_state[0]:
                  ln_state[0] = True
                  emit_ln()

          pend_a = []
          pend_b = []
          emit_qk(0)
          for u in range(nun):
              if u + 1 < nun:
                  emit_qk(u + 1)
              emit_exp(u)
              for hh, ue in [x for x in pend_b if x[1] < u]:
                  emit_norm_b(hh)
                  pend_b.remove((hh, ue))
              for hh in pend_a:
                  emit_norm_a(hh)
                  pend_b.append((hh, u))
              pend_a = []
              emit_pv(u)
              if units[u][2]:
                  pend_a.append(units[u][0])
          for hh, ue in pend_b:
              emit_norm_b(hh)
          for hh in pend_a:
              emit_norm_a(hh)
              emit_norm_b(hh)
          conv_some(1000)
          if not ln_state[0]:
              ln_state[0] = True
              emit_ln()
          if it == ntiles - 1:
              dump("glu", GLU, [R["GLU"]])
              dump("acc", ACC, [R["ACC"]])
              dump("ct", CT, [R["CT"]])
          if it == ntiles - 1:
              dump("att", ATT, [R["ATT"]])
          chk('s5')
          for kc in range(8):
              w, wr = piece(it, P_WO + kc)
              for j in range(NSUB):
                  lhs = ATT[:, kc, j * 128:(j + 1) * 128] if kc < 4 else CT[:, kc - 4, j * 128:(j + 1) * 128]
                  rd = [R["ATT"][2 * kc], R["ATT"][2 * kc + 1]] if kc < 4 else [R["CT"][kc - 4]]
                  for hb in range(2):
                      b = 4 * j + hb
                      S.pe(lambda e, w=w, lhs=lhs, hb=hb, b=b, kc=kc: e.matmul(PSF[:, b, :], lhsT=lhs, rhs=w[:, hb * 512:(hb + 1) * 512],
                                                                               start=(kc == 0), stop=(kc == 7)), reads=rd + [wr], writes=[bank(b)])
          for j in range(NSUB):
              post_norm_residual((4 * j, 4 * j + 1), j, 0)
          if it == ntiles - 1:
              dump("h1", H, [R["H"]])
          S.dve(lambda e: e.memset(SM[:, 60:61], 0.0), writes=[R[n] for n in MIXER_RES + FFN_RES] + [R["SM"][15]])
          chk('s6')
          norm_transpose(V_GFFN, (0, 1))
          S.dve(lambda e: e.tensor_copy(out=PBF, in_=PS_), reads=[R["PSB"]], writes=[R["PBF"]])
          for j in range(NSUB):
              for c in range(2):
                  S.pe(lambda e, j=j, c=c: e.transpose(out=PSB16[:, 4, (c * NSUB + j) * 128:(c * NSUB + j + 1) * 128], in_=PBF[:, j, c * 128:(c + 1) * 128],
                                                       identity=IDB), reads=[R["PBF"], R["CONSTB"]], writes=[bank(4)])
          S.act(lambda e: e.activation(out=PTT.rearrange("p c t -> p (c t)"), in_=PSB16[:, 4, 0:2 * T], func=AF.Copy), reads=[bank(4)], writes=[R["PTT"]])
          ffn_pend = []
          gelu_pend = []
          for c in range(NFC):
              wg_, wgr = piece(it, P_WGU + 2 * c)
              wu_, wur = piece(it, P_WGU + 2 * c + 1)
              bg, bu = 2 + 2 * (c % 3), 3 + 2 * (c % 3)
              s2 = c % 3
              for kc in range(8):
                  S.pe(lambda e, wg_=wg_, kc=kc, bg=bg: e.matmul(PSF[:, bg, 0:T], lhsT=wg_[:, kc * 128:(kc + 1) * 128], rhs=XNT[:, kc, :],
                                                                start=(kc == 0), stop=(kc == 7)), reads=[R["XNT"][kc], wgr], writes=[bank(bg)])
              for kc in range(8):
                  S.pe(lambda e, wu_=wu_, kc=kc, bu=bu: e.matmul(PSF[:, bu, 0:T], lhsT=wu_[:, kc * 128:(kc + 1) * 128], rhs=XNT[:, kc, :],
                                                                start=(kc == 0), stop=(kc == 7)), reads=[R["XNT"][kc], wur], writes=[bank(bu)])
              gs, gsr = GS[s2], R["GS"][s2]
              a1, a1r = A1[s2], R["A1"][s2]
              a2, a2r = A2[s2], R["A2"][s2]
              S.act(lambda e, gs=gs, bg=bg: e.activation(out=gs[:, 2:2 + T], in_=PSF[:, bg, 0:T], func=AF.Copy),
                    reads=[bank(bg)], writes=[gsr])
              for f in gelu_pend:
                  f()
              gshr = R["GSH"][s2]
              S.pool(lambda e, gs=gs, c=c: e.tensor_copy(out=gs[:, 0:2], in_=GH[:, c, :]), reads=[R["GH"][c]], writes=[gshr])
              fw = VEC[:, V_FW + 3 * c:V_FW + 3 * c + 3]
              fb = VEC[:, V_FB + c:V_FB + c + 1]
              S.dve(lambda e, gs=gs, a1=a1, fw=fw, fb=fb: e.tensor_scalar(out=a1, in0=gs[:, 2:2 + T], scalar1=fw[:, 2:3], scalar2=fb, op0=ALU.mult, op1=ALU.add),
                    reads=[gsr, R["VEC"]], writes=[a1r], name="chain")
              S.dve(lambda e, gs=gs, a1=a1, fw=fw: e.scalar_tensor_tensor(out=a1, in0=gs[:, 1:1 + T], scalar=fw[:, 1:2], in1=a1, op0=ALU.mult, op1=ALU.add),
                    reads=[gsr, gshr, R["VEC"], a1r], writes=[a1r], name="chain")
              S.dve(lambda e, gs=gs, a1=a1, fw=fw: e.scalar_tensor_tensor(out=a1, in0=gs[:, 0:T], scalar=fw[:, 0:1], in1=a1, op0=ALU.mult, op1=ALU.add),
                    reads=[gsr, gshr, R["VEC"], a1r], writes=[a1r], name="chain")
              S.pool(lambda e, gs=gs, c=c: e.tensor_copy(out=GH[:, c, :], in_=gs[:, T:T + 2]), reads=[gsr], writes=[R["GH"][c]])
              for f in ffn_pend:
                  f()
              gelu_pend = [lambda a1=a1, a2=a2, a1r=a1r, a2r=a2r: S.act(
                  lambda e: e.activation(out=a2, in_=a1, func=AF.Gelu_apprx_tanh), reads=[a1r], writes=[a2r])]
              ffn_pend = [lambda a2=a2, bu=bu, c=c, a2r=a2r: S.dve(
                  lambda e: e.tensor_tensor(out=HID[:, c, :], in0=a2, in1=PSF[:, bu, 0:T], op=ALU.mult),
                  reads=[a2r, bank(bu)], writes=[R["HID"][c]])]
          for f in gelu_pend:
              f()
          for f in ffn_pend:
              f()
          for c in range(NFC):
              w, wr = piece(it, P_WD + c)
              for j in range(NSUB):
                  for hb in range(2):
                      b = 2 * j + hb
                      S.pe(lambda e, w=w, c=c, j=j, hb=hb, b=b: e.matmul(PSF[:, b, :], lhsT=HID[:, c, j * 128:(j + 1) * 128], rhs=w[:, hb * 512:(hb + 1) * 512],
                                                                         start=(c == 0), stop=(c == NFC - 1)), reads=[R["HID"][c], wr], writes=[bank(b)])
          EB = (4, 6)
          GB = (2, 0)
          for kc in range(2):
              w, wr = piece(it, P_WPP + kc)
              for j in range(NSUB):
                  for hb in range(2):
                      S.pe(lambda e, w=w, kc=kc, j=j, hb=hb: e.matmul(PSF[:, EB[j] + hb, :], lhsT=PTT[:, kc, j * 128:(j + 1) * 128], rhs=w[:, hb * 512:(hb + 1) * 512],
                                                                     start=(kc == 0), stop=(kc == 1)), reads=[R["PTT"], wr], writes=[bank(EB[j] + hb)])
          for j in range(NSUB):
              post_norm_residual((2 * j, 2 * j + 1), j, 1)
          for j in range(NSUB):
              eb = EB[j]
              src = PSF[:, eb:eb + 2, :].rearrange("p b n -> p (b n)")
              S.act(lambda e, src=src, j=j: e.activation(out=JUNK, in_=src, func=AF.Square, accum_out=SM[:, 24 + j:25 + j]),
                    reads=[bank(eb), bank(eb + 1)], writes=[R["JUNK"], R["SM"][9 + 2 * j]])
              rstd_from_ss(SM[:, 24 + j:25 + j], SM[:, 26 + j:27 + j], 1024.0, [R["SM"][9 + 2 * j]], [R["SM"][10 + 2 * j]])
          S.dve(lambda e: e.scalar_tensor_tensor(out=E, in0=PSF[:, EB[0]:EB[0] + 2, :].rearrange("p b n -> p (b n)"), scalar=SM[:, 26:27], in1=GROW[:, 2, :],
                                                 op0=ALU.mult, op1=ALU.mult),
                reads=[bank(EB[0]), bank(EB[0] + 1), R["SM"][10], R["GROW"]], writes=[R["E"]])
          if it == ntiles - 1:
              dump("h2", H, [R["H"]])
          chk('s7')
          norm_transpose(None, (0, 1))
          for kc in range(8):
              w, wr = piece(it, P_WPG + kc)
              for j in range(NSUB):
                  for hb in range(2):
                      S.pe(lambda e, w=w, kc=kc, j=j, hb=hb: e.matmul(PSF[:, GB[j] + hb, :], lhsT=XNT[:, kc, j * 128:(j + 1) * 128], rhs=w[:, hb * 512:(hb + 1) * 512],
                                                                     start=(kc == 0), stop=(kc == 7)), reads=[R["XNT"][kc], wr], writes=[bank(GB[j] + hb)])
          for j in range(NSUB):
              eb, gb = EB[j], GB[j]
              if j > 0:
                  S.dve(lambda e, eb=eb, j=j: e.scalar_tensor_tensor(out=E, in0=PSF[:, eb:eb + 2, :].rearrange("p b n -> p (b n)"), scalar=SM[:, 26 + j:27 + j],
                                                                     in1=GROW[:, 2, :], op0=ALU.mult, op1=ALU.mult),
                        reads=[bank(eb), bank(eb + 1), R["SM"][10 + 2 * j], R["GROW"]], writes=[R["E"]])
              S.act(lambda e, gb=gb: e.activation(out=SG, in_=PSF[:, gb:gb + 2, :].rearrange("p b n -> p (b n)"), func=AF.Sigmoid),
                    reads=[bank(gb), bank(gb + 1)], writes=[R["SG"]])
              S.dve(lambda e: e.tensor_tensor(out=E, in0=E, in1=SG, op=ALU.mult), reads=[R["E"], R["SG"]], writes=[R["E"]])
              S.dve(lambda e, j=j: e.tensor_tensor(out=H[:, j, :], in0=H[:, j, :], in1=E, op=ALU.add), reads=[R["H"][j], R["E"]], writes=[R["H"][j]])
              st = S.dma("pool", lambda e, t0=t0, j=j: e.dma_start(out=y_d[t0 + j * 128:t0 + (j + 1) * 128, :], in_=H[:, j, :]),
                         reads=[R["H"][j]], writes=[R["Y"]], key=dkey("ystore%d" % j))
              final_ops.append(st)
          S.dve(lambda e: e.memset(SM[:, 61:62], 0.0), writes=[R[n] for n in MIXER_RES + FFN_RES] + [R["SM"][15]])
    except _Stop:
        pass
    if not final_ops:
        final_ops.append(S.dma("sp", lambda e: e.dma_start(out=y_d[0:128, 0:256], in_=VEC), reads=[R["VEC"], R["H"], R["XNT"], R["WS"], R["KT"], R["VP"], R["QT"], R["CT"], R["ATT"], R["TAB"], R["POSF"], R["QAT"], R["QKA"]],
                               writes=[R["Y"]], key=dkey("ystore")))
    S.emit(final_wait_ops=final_ops[-2:] + [op for op in S.ops if op.is_dma and op.key.name.startswith("kdbg")])
    nc._sched = S
    nc._sbuf_used = sbuf_used
    return nc


_NC_CACHE = {}


def make_in_maps(inputs):
    x = np.asarray(inputs["x"], np.float32)
    p = np.asarray(inputs["p"], np.float32)[0]
    pos = np.asarray(inputs["positions"]).astype(np.int32)
    vec = pack_vecs(inputs)
    grow = np.ascontiguousarray(np.broadcast_to(
        np.stack([np.asarray(inputs["g_mix_post"][0], np.float32), np.asarray(inputs["g_ffn_post"][0], np.float32),
                  np.asarray(inputs["g_ple"][0], np.float32)])[None], (128, 3, 1024)))
    consts = make_consts()
    shared = {
        "w_in": np.ascontiguousarray(inputs["w_in"][0], dtype=np.float32),
        "w_q_b": np.ascontiguousarray(inputs["w_q_b"][0], dtype=np.float32),
        "w_kv_b": np.ascontiguousarray(inputs["w_kv_b"][0], dtype=np.float32),
        "w_o": np.ascontiguousarray(inputs["w_o"][0], dtype=np.float32),
        "w_ffn_gate": np.ascontiguousarray(inputs["w_ffn_gate"][0], dtype=np.float32),
        "w_ffn_up": np.ascontiguousarray(inputs["w_ffn_up"][0], dtype=np.float32),
        "w_ffn_down": np.ascontiguousarray(inputs["w_ffn_down"][0], dtype=np.float32),
        "w_ple_proj": np.ascontiguousarray(inputs["w_ple_proj"][0], dtype=np.float32),
        "w_ple_gate": np.ascontiguousarray(inputs["w_ple_gate"][0], dtype=np.float32),
        "vecs": vec, "grow": grow, "consts": consts,
    }
    maps = []
    for b in range(8):
        m = dict(shared)
        m["x"] = np.ascontiguousarray(x[b])
        m["p"] = np.ascontiguousarray(p[b])
        m["pos"] = np.ascontiguousarray(pos[b].reshape(32, 128))
        maps.append(m)
    return maps


def kernel(**inputs):
    if "nc" not in _NC_CACHE:
        _NC_CACHE["nc"] = build_nc(same_engine_sync=True)
    nc = _NC_CACHE["nc"]
    maps = make_in_maps(inputs)
    res = run_bass_kernel_spmd(nc, maps, core_ids=list(range(8)))
    out = np.stack([np.asarray(r["y"], np.float32) for r in res.results], axis=0)
    return out
```

```python
import contextlib
import numpy as np
import concourse.bass as bass
import concourse.mybir as mybir
from concourse.ap import AP
from concourse.bass_utils import run_bass_kernel_spmd

F32 = mybir.dt.float32
BF16 = mybir.dt.bfloat16
I32 = mybir.dt.int32
AF = mybir.ActivationFunctionType
ALU = mybir.AluOpType
AX = mybir.AxisListType

SEQ = 4096
DM = 1024
T = 256
NT = SEQ // T
NSUB = T // 128
EPS = 1e-6
ATT_SCALE = 96.0 ** -0.5
DFF = 2816
NFC = DFF // 128
NSLOT = 12
MASKVAL = -30000.0

COMPUTE = ("pe", "act", "dve", "pool")


class Res:
    _n = 0

    def __init__(self, name, nsub=1):
        self.name = name
        self.nsub = nsub
        self.id = Res._n
        Res._n += 1
        self.exclusive = False

    def keys(self):
        return [(self.id, i) for i in range(self.nsub)]

    def __getitem__(self, i):
        return SubRes(self, i)


class SubRes:
    def __init__(self, res, i):
        self.res = res
        self.i = i
        self.exclusive = res.exclusive

    def keys(self):
        return [(self.res.id, self.i)]


class Op:
    __slots__ = ("idx", "eng", "fn", "is_dma", "key", "deps", "sig", "sigval", "dma_sem",
                 "dma_target", "name", "snap", "chain")


class Sched:
    def __init__(self, nc, same_engine_sync=False):
        self.nc = nc
        self.ops = []
        self.last_writers = {}
        self.readers = {}
        self.same_engine_sync = same_engine_sync

    def _add(self, eng, fn, reads, writes, is_dma=False, key=None, name=""):
        op = Op()
        op.idx = len(self.ops)
        op.eng = eng
        op.fn = fn
        op.is_dma = is_dma
        op.key = key
        op.name = name
        op.sig = False
        op.chain = name == "chain"
        op.name = getattr(self, "cur", "")
        op.snap = [(n, c.cell_contents) for n, c in zip(fn.__code__.co_freevars, fn.__closure__ or ())]
        deps = set()
        rk = []
        wk = []
        for r in reads:
            if r.exclusive:
                wk += r.keys()
            else:
                rk += r.keys()
        for w in writes:
            wk += w.keys()
        for k in rk:
            for w in self.last_writers.get(k, ()):
                deps.add(w)
        for k in wk:
            for w in self.last_writers.get(k, ()):
                deps.add(w)
            for r in self.readers.get(k, ()):
                deps.add(r)
        op.deps = deps
        for k in rk:
            self.readers.setdefault(k, []).append(op.idx)
        for k in wk:
            self.last_writers[k] = [op.idx]
            self.readers[k] = []
        self.ops.append(op)
        return op

    def pe(self, fn, reads=(), writes=(), name=""):
        return self._add("pe", fn, reads, writes, name=name)

    def act(self, fn, reads=(), writes=(), name=""):
        return self._add("act", fn, reads, writes, name=name)

    def dve(self, fn, reads=(), writes=(), name=""):
        return self._add("dve", fn, reads, writes, name=name)

    def pool(self, fn, reads=(), writes=(), name=""):
        return self._add("pool", fn, reads, writes, name=name)

    def dma(self, queue, fn, reads=(), writes=(), key=None, name=""):
        return self._add(queue, fn, reads, writes, is_dma=True, key=key, name=name)

    def _skip(self, dop, op):
        if dop.is_dma or op.is_dma:
            return False
        if dop.eng != op.eng:
            return False
        if dop.eng == "pe":
            return True
        if dop.chain and op.chain and self.same_engine_sync is not True:
            return True
        ses = self.same_engine_sync
        if isinstance(ses, (set, frozenset, list, tuple)):
            return dop.eng not in ses
        return not ses

    def emit(self, final_wait_ops=()):
        nc = self.nc
        ops = self.ops
        for op in ops:
            for d in op.deps:
                dop = ops[d]
                if dop.is_dma or self._skip(dop, op):
                    continue
                dop.sig = True
        for op in final_wait_ops:
            if not op.is_dma:
                op.sig = True
        cnt = {e: 0 for e in COMPUTE}
        for op in ops:
            if not op.is_dma and op.sig:
                cnt[op.eng] += 1
                op.sigval = cnt[op.eng]
        stack = contextlib.ExitStack()
        esem = {e: stack.enter_context(nc.semaphore("s_" + e)) for e in COMPUTE}
        dsem = {}
        dcnt = {}
        for op in ops:
            if op.is_dma:
                kid = op.key.id
                if kid not in dsem:
                    dsem[kid] = stack.enter_context(nc.semaphore("d_%s" % op.key.name))
                    dcnt[kid] = 0
                dcnt[kid] += 16
                op.dma_sem = dsem[kid]
                op.dma_target = dcnt[kid]
        self.n_sems = len(esem) + len(dsem)
        streams = {"pe": [], "act": [], "dve": [], "pool": [], "sp": []}
        for op in ops:
            streams[op.eng].append(op)
        block = stack.enter_context(nc.Block())

        def run_stream(eng_name, eng):
            seen = {}
            for op in streams[eng_name]:
                waits = {}
                for d in op.deps:
                    dop = ops[d]
                    if dop.is_dma:
                        s, v = dop.dma_sem, dop.dma_target
                    else:
                        if self._skip(dop, op):
                            continue
                        s, v = esem[dop.eng], dop.sigval
                    if seen.get(s.num, 0) >= v:
                        continue
                    if waits.get(s.num, (None, 0))[1] < v:
                        waits[s.num] = (s, v)
                for s, v in waits.values():
                    eng.wait_ge(s, v)
                    seen[s.num] = v
                for (n, v0), c in zip(op.snap, op.fn.__closure__ or ()):
                    v1 = c.cell_contents
                    if not (v1 is v0 or (isinstance(v0, (int, float, bool, str, tuple)) and v0 == v1)):
                        raise RuntimeError("late-bound closure variable %r changed in op %d (%s)" % (n, op.idx, op.fn.__code__.co_firstlineno))
                ins = op.fn(eng)
                if op.is_dma:
                    ins.then_inc(op.dma_sem, 16)
                elif op.sig:
                    ins.then_inc(esem[op.eng], 1)
            if eng_name == "sp":
                for op in final_wait_ops:
                    if op.is_dma:
                        eng.wait_ge(op.dma_sem, op.dma_target)
                    else:
                        eng.wait_ge(esem[op.eng], op.sigval)

        @block.tensor
        def _(e):
            run_stream("pe", e)

        @block.scalar
        def _(e):
            run_stream("act", e)

        @block.vector
        def _(e):
            run_stream("dve", e)

        @block.gpsimd
        def _(e):
            run_stream("pool", e)

        @block.sync
        def _(e):
            run_stream("sp", e)

        stack.close()


V_GPRE, V_GQA, V_GKVA, V_GFFN, V_CB, V_LNG, V_LNB, V_CW, V_FW, V_FB, NV = 0, 8, 11, 13, 21, 25, 29, 33, 157, 223, 245


def _fm(v):
    v = np.asarray(v, np.float32)
    return np.ascontiguousarray(v.reshape(-1, 128).T)


def pack_vecs(inp):
    vec = np.zeros((128, 256), np.float32)
    vec[:, V_GPRE:V_GPRE + 8] = _fm(inp["g_mix_pre"][0])
    vec[:, V_GQA:V_GQA + 3] = _fm(inp["g_q_a"][0])
    vec[:, V_GKVA:V_GKVA + 2] = _fm(inp["g_kv_a"][0])
    vec[:, V_GFFN:V_GFFN + 8] = _fm(inp["g_ffn_pre"][0])
    vec[:, V_CB:V_CB + 4] = _fm(inp["conv_b"][0])
    vec[:, V_LNG:V_LNG + 4] = _fm(inp["conv_ln_g"][0])
    vec[:, V_LNB:V_LNB + 4] = _fm(inp["conv_ln_b"][0])
    cw = np.asarray(inp["conv_w"][0], np.float32)
    for cg in range(4):
        vec[:, V_CW + cg * 31:V_CW + (cg + 1) * 31] = cw[:, cg * 128:(cg + 1) * 128].T
    fw = np.asarray(inp["ffn_dw_w"][0], np.float32)
    for c in range(NFC):
        vec[:, V_FW + c * 3:V_FW + (c + 1) * 3] = fw[:, c * 128:(c + 1) * 128].T
    vec[:, V_FB:V_FB + NFC] = _fm(inp["ffn_dw_b"][0])
    return vec


def make_consts():
    c = np.zeros((128, 512), np.float32)
    c[:, 0:128] = np.eye(128, dtype=np.float32)
    k = np.arange(128)[:, None]
    q = np.arange(128)[None, :]
    c[:, 128:256] = np.where(k <= q, 0.0, MASKVAL).astype(np.float32)
    inv_freq = (10000.0 ** (-np.arange(0, 32, 2, dtype=np.float32) / 32.0)).astype(np.float32)
    c[:, 256:272] = inv_freq[None, :]
    return c


P_WINT = 0
P_WQB = 8
P_WKVB = 11
P_WINF = 13
P_WO = 21
P_WGU = 29
P_WD = 73
P_WPP = 95
P_WPG = 97
NPIECE = 105


def build_nc(ntiles=NT, dbg=None, same_engine_sync=False, stop=None, noconv=False):
    dbg = dbg or {}
    nc = bass.Bass("TRN2", target_bir_lowering=False)
    S = Sched(nc, same_engine_sync=same_engine_sync)

    def din(name, shape, dt=F32):
        return nc.dram_tensor(name, list(shape), dt, kind="ExternalInput").ap()

    x_d = din("x", [SEQ, DM])
    p_d = din("p", [SEQ, 256])
    pos_d = din("pos", [32, 128], I32)
    w_in_d = din("w_in", [1024, 1696])
    w_qb_d = din("w_q_b", [384, 768])
    w_kvb_d = din("w_kv_b", [256, 1024])
    w_o_d = din("w_o", [1024, 1024])
    w_g_d = din("w_ffn_gate", [1024, DFF])
    w_u_d = din("w_ffn_up", [1024, DFF])
    w_d_d = din("w_ffn_down", [DFF, 1024])
    w_pp_d = din("w_ple_proj", [256, 1024])
    w_pg_d = din("w_ple_gate", [1024, 1024])
    vec_d = din("vecs", [128, 256])
    grow_d = din("grow", [128, 3, 1024])
    const_d = din("consts", [128, 512])
    y_d = nc.dram_tensor("y", [SEQ, DM], F32, kind="ExternalOutput").ap()
    ws_d = nc.dram_tensor("wscratch", [NPIECE, 128, 1024], BF16, kind="Internal").ap()
    dbg_d = {k: nc.dram_tensor("dbg_" + k, list(shp), F32, kind="ExternalOutput").ap()
             for k, shp in dbg.items()}

    off = [16512]

    def sb(name, shape, dt, at=None):
        n = 1
        for s in shape[1:]:
            n *= s
        nb = n * (4 if dt in (F32, I32) else 2)
        nb = (nb + 31) // 32 * 32
        if at is None:
            at = off[0]
            off[0] += nb
        assert at + nb <= 229344, (name, at, nb)
        return nc.alloc_sbuf_tensor_at(name, list(shape), dt, offset=at).ap()

    KT = sb("KT", [98, 8, SEQ], BF16)
    VP = sb("VP", [128, 32, 8, 65], BF16)
    GROW = sb("GROW", [128, 3, 1024], F32)
    RING = [sb("RING%d" % i, [128, 1024], BF16) for i in range(NSLOT)]
    VEC = sb("VEC", [128, 256], F32)
    CST = sb("CST", [128, 512], F32)
    IDB = sb("IDB", [128, 128], BF16)
    MASK = sb("MASK", [128, 128], BF16)
    ONESB = sb("ONESB", [128, 128], BF16)
    ONESF = sb("ONESF", [128, 64], F32)
    SEL = sb("SEL", [128, 64], BF16)
    COS = sb("COS", [128, 32, 16], F32)
    SIN = sb("SIN", [128, 32, 16], F32)
    NSIN = sb("NSIN", [128, 32, 16], F32)
    POSF = sb("POSF", [128, 32], F32)
    GH = sb("GH", [128, NFC, 2], F32)
    SM = sb("SM", [128, 64], F32)
    H = sb("H", [128, NSUB, 1024], F32)
    XNT = sb("XNT", [128, 8, T], BF16)
    XB = sb("XB", [128, NSUB, 1024], BF16)
    JUNK = sb("JUNK", [128, 1024], BF16)
    TMPF = sb("TMPF", [128, 1024], F32)
    AUGQ = sb("AUGQ", [128, 8, 98], BF16)
    AUGK = sb("AUGK", [128, 8, 98], BF16)
    GLU = sb("GLU", [128, 4, 30 + T], F32)
    PS_ = sb("PSB", [128, NSUB, 256], F32)
    arena0 = off[0]
    QAT = sb("QAT", [128, 5, T], BF16)
    QT = sb("QT", [98, 8, T], BF16)
    CT = sb("CT", [128, 4, T], BF16)
    PT = [sb("PT%d" % i, [128, 1024], BF16) for i in range(2)]
    ATT = sb("ATT", [128, 4, T], BF16)
    OS = [sb("OS%d" % i, [64, T], F32) for i in range(2)]
    RDEN = sb("RDEN", [65, T], F32)
    RHL = sb("RHL", [128, 2, T], BF16)
    sub0 = off[0]
    QKA = sb("QKA", [128, NSUB, 640], BF16)
    KR = sb("KR", [128, NSUB, 32], F32)
    QR = sb("QR", [128, 8, 32], F32)
    RA = sb("RA", [128, 8, 32], F32)
    RB = sb("RB", [128, 8, 32], F32)
    SQ = sb("SQ", [128, 768], F32)
    VS = sb("VS", [128, 8, 64], F32)
    NRM = sb("NRM", [128, 32], F32)
    sub1 = off[0]
    off[0] = sub0
    SIG = sb("SIG", [128, 2, T], F32)
    ACC = sb("ACC", [128, 4, T], F32)
    CBF = sb("CBF", [128, 4, T], BF16)
    CSQ = sb("CSQ", [128, 4, T], BF16)
    LNM = sb("LNM", [128, T], F32)
    LNR = sb("LNR", [128, T], F32)
    LNT = sb("LNT", [128, 2, T], F32)
    off[0] = max(off[0], sub1)
    arena1 = off[0]
    off[0] = arena0
    HID = sb("HID", [128, NFC, T], BF16)
    GS = [sb("GS%d" % i, [128, 2 + T], F32) for i in range(3)]
    A1 = [sb("A1_%d" % i, [128, T], F32) for i in range(3)]
    A2 = [sb("A2_%d" % i, [128, T], F32) for i in range(3)]
    PBF = sb("PBF", [128, NSUB, 256], BF16)
    PTT = sb("PTT", [128, 2, T], BF16)
    E = sb("E", [128, 1024], F32)
    SG = sb("SG", [128, 1024], F32)
    arena2 = off[0]
    off[0] = max(arena1, arena2)
    sbuf_used = off[0]

    PSF = nc.alloc_psum_tensor("PS", [128, 8, 512], F32).ap()
    PSB16 = PSF.bitcast(BF16)

    R = {}
    for n, ns in [("KT", 32), ("VP", 32), ("GROW", 1), ("VEC", 1), ("CST", 1), ("CONSTB", 1), ("TAB", 1),
                  ("POSF", 1), ("GH", NFC), ("SM", 16), ("H", NSUB), ("XNT", 8), ("XB", NSUB), ("JUNK", 1),
                  ("TMPF", 1), ("ARENA", 1), ("QKA", NSUB), ("QAT", 1), ("KR", NSUB), ("QR", 1), ("RA", 1),
                  ("RB", 1), ("SQ", 1), ("VS", 1), ("NRM", 1), ("AUGQ", 1), ("AUGK", 1), ("QT", 1),
                  ("GLU", 4), ("SIG", 2), ("ACC", 4), ("CBF", 4), ("CSQ", 4), ("CT", 4), ("LN", 1),
                  ("PT", 3), ("ATT", 8), ("OS", 2), ("RDEN", 1), ("HID", NFC), ("GS", 3), ("GSH", 3), ("A1", 3), ("A2", 3),
                  ("PSB", 1), ("PBF", 1), ("PTT", 1), ("E", 1), ("SG", 1), ("PSUM", 8), ("RING", NSLOT),
                  ("X", 1), ("Y", 1), ("WS", NPIECE), ("DBG", 1)]:
        R[n] = Res(n, ns)
    R["PSUM"].exclusive = True
    DKEY = {}

    def dkey(n):
        if n not in DKEY:
            DKEY[n] = Res("k" + n)
        return DKEY[n]

    MIXER_RES = ["QKA", "QAT", "KR", "QR", "RA", "RB", "SQ", "VS", "NRM", "QT", "SIG",
                 "ACC", "CBF", "CSQ", "CT", "LN", "PT", "ATT", "OS", "RDEN"]
    S3_RES = ["QKA", "KR", "QR", "RA", "RB", "SQ", "VS", "NRM"]
    S4_RES = ["SIG", "ACC", "CBF", "CSQ", "LN"]
    FFN_RES = ["HID", "GS", "GSH", "A1", "A2", "PBF", "PTT", "E", "SG"]

    def bank(b):
        return R["PSUM"][b]

    def bc(ap, dims):
        return AP(ap.tensor, ap.offset, [list(ap.ap[0])] + [list(d) for d in dims])

    S.dma("sp", lambda e: e.dma_start(out=CST, in_=const_d), writes=[R["CST"]], key=dkey("cst"))
    S.dma("sp", lambda e: e.dma_start(out=VEC, in_=vec_d), writes=[R["VEC"]], key=dkey("vec"))
    S.dma("sp", lambda e: e.dma_start(out=GROW, in_=grow_d), writes=[R["GROW"]], key=dkey("grow"))
    S.dve(lambda e: e.tensor_copy(out=IDB, in_=CST[:, 0:128]), reads=[R["CST"]], writes=[R["CONSTB"]])
    S.dve(lambda e: e.tensor_copy(out=MASK, in_=CST[:, 128:256]), reads=[R["CST"]], writes=[R["CONSTB"]])
    S.dve(lambda e: e.memset(ONESB, 1.0), writes=[R["CONSTB"]])
    S.dve(lambda e: e.memset(ONESF, 1.0), writes=[R["CONSTB"]])
    S.dve(lambda e: e.memset(SEL, 0.0), writes=[R["CONSTB"]])
    S.dve(lambda e: e.memset(SEL[64:65, :], 1.0), writes=[R["CONSTB"]])
    S.dve(lambda e: e.memset(AUGQ, 1.0), writes=[R["AUGQ"]])
    S.dve(lambda e: e.memset(AUGK, 1.0), writes=[R["AUGK"]])
    S.dve(lambda e: e.memset(GH, 0.0), writes=[R["GH"]])
    S.dve(lambda e: e.memset(SM[:, 48:49], EPS), writes=[R["SM"][14]])
    S.dve(lambda e: e.memset(GLU, 0.0), writes=[R["GLU"]])
    POSI = bc(TMPF, [[1, 128]]).bitcast(I32)
    S.dma("sp", lambda e: e.dma_start(out=POSI[0:32, :], in_=pos_d), writes=[R["TMPF"]], key=dkey("pos"))
    S.dve(lambda e: e.tensor_copy(out=TMPF[0:32, 128:256], in_=POSI[0:32, :]), reads=[R["TMPF"]], writes=[R["TMPF"]])
    S.pe(lambda e: e.transpose(out=PSF[:, 0, 0:32], in_=TMPF[0:32, 128:256], identity=CST[0:32, 0:32]),
         reads=[R["TMPF"], R["CST"]], writes=[bank(0)])
    S.dve(lambda e: e.tensor_copy(out=POSF, in_=PSF[:, 0, 0:32]), reads=[bank(0)], writes=[R["POSF"]])
    TT = TMPF[:, 0:512]
    TK = TMPF[:, 512:1024]
    TKI = TK.bitcast(I32)
    INV2PI = float(1.0 / (2.0 * np.pi))

    def table(dst, shift, negate):
        S.dve(lambda e: e.tensor_tensor(out=bc(TT, [[16, 32], [1, 16]]), in0=bc(POSF, [[1, 32], [0, 16]]),
                                        in1=bc(CST[:, 256:272], [[0, 32], [1, 16]]), op=ALU.mult),
              reads=[R["POSF"], R["CST"]], writes=[R["TMPF"]])
        S.dve(lambda e: e.tensor_scalar(out=TT, in0=TT, scalar1=INV2PI, scalar2=shift, op0=ALU.mult, op1=ALU.add),
              reads=[R["TMPF"]], writes=[R["TMPF"]])
        S.dve(lambda e: e.tensor_copy(out=TKI, in_=TT), reads=[R["TMPF"]], writes=[R["TMPF"]])
        S.dve(lambda e: e.tensor_copy(out=TK, in_=TKI), reads=[R["TMPF"]], writes=[R["TMPF"]])
        S.dve(lambda e: e.tensor_tensor(out=TT, in0=TT, in1=TK, op=ALU.subtract), reads=[R["TMPF"]], writes=[R["TMPF"]])
        S.dve(lambda e: e.tensor_scalar(out=TT, in0=TT, scalar1=0.49999, scalar2=-0.49999, op0=ALU.min, op1=ALU.max),
              reads=[R["TMPF"]], writes=[R["TMPF"]])
        sc = float(-2.0 * np.pi) if negate else float(2.0 * np.pi)
        S.act(lambda e: e.activation(out=dst.rearrange("p a b -> p (a b)"), in_=TT, func=AF.Sin, scale=sc),
              reads=[R["TMPF"]], writes=[R["TAB"]])

    table(SIN, 0.0, False)
    table(NSIN, 0.0, True)
    table(COS, 0.25, False)

    def conv_dma(out_ap, in_ap, pieces, name):
        S.dma("pool", lambda e: e.dma_start(out=out_ap, in_=in_ap), writes=[R["WS"][i] for i in pieces],
              key=dkey("cv" + name), name=name)

    conv_dma(ws_d[P_WINT:P_WINT + 8, :, 0:672], w_in_d[:, 0:672].rearrange("(kc p) n -> kc p n", p=128),
             range(P_WINT, P_WINT + 8), "wint")
    conv_dma(ws_d[P_WQB:P_WQB + 3, :, 0:768], w_qb_d.rearrange("(kc p) n -> kc p n", p=128), range(P_WQB, P_WQB + 3), "wqb")
    conv_dma(ws_d[P_WKVB:P_WKVB + 2, :, :], w_kvb_d.rearrange("(kc p) n -> kc p n", p=128), range(P_WKVB, P_WKVB + 2), "wkvb")
    for cg in range(4):
        for g in range(2):
            col = 672 + g * 512 + cg * 128
            conv_dma(ws_d[P_WINF + 2 * cg + g].rearrange("p (kc m) -> p kc m", kc=8),
                     w_in_d[:, col:col + 128].rearrange("(kc p) m -> p kc m", p=128), [P_WINF + 2 * cg + g], "winf%d%d" % (cg, g))
    conv_dma(ws_d[P_WO:P_WO + 8], w_o_d.rearrange("(kc p) n -> kc p n", p=128), range(P_WO, P_WO + 8), "wo")
    for c in range(NFC):
        conv_dma(ws_d[P_WGU + 2 * c].rearrange("p (kc m) -> p kc m", kc=8),
                 w_g_d[:, c * 128:(c + 1) * 128].rearrange("(kc p) m -> p kc m", p=128), [P_WGU + 2 * c], "wg%d" % c)
        conv_dma(ws_d[P_WGU + 2 * c + 1].rearrange("p (kc m) -> p kc m", kc=8),
                 w_u_d[:, c * 128:(c + 1) * 128].rearrange("(kc p) m -> p kc m", p=128), [P_WGU + 2 * c + 1], "wu%d" % c)
    for half in range(2):
        conv_dma(ws_d[P_WD + 11 * half:P_WD + 11 * (half + 1)],
                 w_d_d[half * 1408:(half + 1) * 1408, :].rearrange("(c p) n -> c p n", p=128),
                 range(P_WD + 11 * half, P_WD + 11 * (half + 1)), "wd%d" % half)
    conv_dma(ws_d[P_WPP:P_WPP + 2], w_pp_d.rearrange("(kc p) n -> kc p n", p=128), range(P_WPP, P_WPP + 2), "wpp")
    conv_dma(ws_d[P_WPG:P_WPG + 8], w_pg_d.rearrange("(kc p) n -> kc p n", p=128), range(P_WPG, P_WPG + 8), "wpg")

    ring_state = {"issued": 0}

    def piece(tile_i, idx):
        g = tile_i * NPIECE + idx
        upto = min(g + NSLOT - 2, ntiles * NPIECE - 1)
        while ring_state["issued"] <= upto:
            gi = ring_state["issued"]
            sl = gi % NSLOT
            pi = gi % NPIECE
            ncol = 672 if pi < P_WINT + 8 else (768 if pi < P_WQB + 3 else 1024)
            S.dma("sp", lambda e, sl=sl, pi=pi, ncol=ncol: e.dma_start(out=RING[sl][:, 0:ncol], in_=ws_d[pi, :, 0:ncol]),
                  reads=[R["WS"][pi]], writes=[R["RING"][sl]], key=dkey("ring%d" % sl), name="ld%d" % pi)
            ring_state["issued"] += 1
        sl = g % NSLOT
        return RING[sl], R["RING"][sl]

    def rstd_from_ss(ss_ap, out_ap, n, reads, writes):
        S.act(lambda e: e.activation(out=out_ap, in_=ss_ap, func=AF.Ln, scale=1.0 / n, bias=SM[:, 48:49]),
              reads=reads + [R["SM"][14]], writes=writes)
        S.act(lambda e: e.activation(out=out_ap, in_=out_ap, func=AF.Exp, scale=-0.5), reads=writes, writes=writes)

    def norm_transpose(gcol, tb):
        for j in range(NSUB):
            if gcol is not None:
                S.act(lambda e, j=j: e.activation(out=JUNK, in_=H[:, j, :], func=AF.Square, accum_out=SM[:, j:j + 1]),
                      reads=[R["H"][j]], writes=[R["JUNK"], R["SM"][0]])
        if gcol is not None:
            rstd_from_ss(SM[:, 0:NSUB], SM[:, 2:2 + NSUB], 1024.0, [R["SM"][0]], [R["SM"][1]])
        for j in range(NSUB):
            if gcol is not None:
                S.dve(lambda e, j=j: e.tensor_scalar(out=XB[:, j, :], in0=H[:, j, :], scalar1=SM[:, 2 + j:3 + j], scalar2=None,
                                                     op0=ALU.mult), reads=[R["H"][j], R["SM"][1]], writes=[R["XB"][j]])
            else:
                S.dve(lambda e, j=j: e.tensor_copy(out=XB[:, j, :], in_=H[:, j, :]), reads=[R["H"][j]], writes=[R["XB"][j]])
            b = tb[j]
            for c in range(8):
                S.pe(lambda e, j=j, c=c, b=b: e.transpose(out=PSB16[:, b, c * 128:(c + 1) * 128], in_=XB[:, j, c * 128:(c + 1) * 128],
                                                          identity=IDB), reads=[R["XB"][j], R["CONSTB"]], writes=[bank(b)])
            if gcol is not None:
                S.dve(lambda e, j=j, b=b: e.tensor_tensor(out=XNT[:, :, j * 128:(j + 1) * 128],
                                                          in0=PSB16[:, b, :].rearrange("p (c t) -> p c t", c=8),
                                                          in1=bc(VEC[:, gcol:gcol + 8], [[1, 8], [0, 128]]), op=ALU.mult),
                      reads=[bank(b), R["VEC"]], writes=[R["XNT"][c] for c in range(8)])
            else:
                S.act(lambda e, j=j, b=b: e.activation(out=XNT[:, :, j * 128:(j + 1) * 128],
                                                       in_=PSB16[:, b, :].rearrange("p (c t) -> p c t", c=8), func=AF.Copy),
                      reads=[bank(b)], writes=[R["XNT"][c] for c in range(8)])

    def post_norm_residual(banks2, j, grow_i):
        b0, b1 = banks2
        src = PSF[:, b0:b0 + 2, :].rearrange("p b n -> p (b n)")
        S.act(lambda e: e.activation(out=JUNK, in_=src, func=AF.Square, accum_out=SM[:, 4 + j:5 + j]),
              reads=[bank(b0), bank(b1)], writes=[R["JUNK"], R["SM"][2 + j]])
        rstd_from_ss(SM[:, 4 + j:5 + j], SM[:, 6 + j:7 + j], 1024.0, [R["SM"][2 + j]], [R["SM"][4 + j]])
        S.dve(lambda e: e.scalar_tensor_tensor(out=TMPF, in0=src, scalar=SM[:, 6 + j:7 + j], in1=GROW[:, grow_i, :],
                                               op0=ALU.mult, op1=ALU.mult),
              reads=[bank(b0), bank(b1), R["SM"][4 + j], R["GROW"]], writes=[R["TMPF"]])
        S.dve(lambda e: e.tensor_tensor(out=H[:, j, :], in0=H[:, j, :], in1=TMPF, op=ALU.add),
              reads=[R["H"][j], R["TMPF"]], writes=[R["H"][j]])

    def dump(name, ap, reads):
        if name in dbg_d:
            S.dma("pool", lambda e: e.dma_start(out=dbg_d[name], in_=ap), reads=reads, writes=[R["DBG"]], key=dkey("dbg" + name))

    final_ops = []

    class _Stop(Exception):
        pass

    def chk(name):
        S.cur = name + "+"
        if stop == name:
            raise _Stop()
    try:
      chk("pro")
      for it in range(ntiles):
          t0 = it * T
          chk('s0')
          for j in range(NSUB):
              S.dma("sp", lambda e, t0=t0, j=j: e.dma_start(out=H[:, j, :], in_=x_d[t0 + j * 128:t0 + (j + 1) * 128, :]),
                    reads=[R["X"]], writes=[R["H"][j]], key=dkey("xload%d" % j))
          S.dma("sp", lambda e, t0=t0: e.dma_start(out=PS_, in_=p_d[t0:t0 + T, :].rearrange("(j p) d -> p j d", p=128)),
                reads=[R["X"]], writes=[R["PSB"]], key=dkey("pload"))
          for n in MIXER_RES:
              pass
          norm_transpose(V_GPRE, (0, 1))
          if it == ntiles - 1:
              dump("xnt", XNT, [R["XNT"]])
          chk('s1')
          for kc in range(8):
              w, wr = piece(it, P_WINT + kc)
              for j in range(NSUB):
                  bA, bB = 2 + 2 * j, 3 + 2 * j
                  S.pe(lambda e, w=w, j=j, kc=kc, bA=bA: e.matmul(PSF[:, bA, :], lhsT=XNT[:, kc, j * 128:(j + 1) * 128], rhs=w[:, 0:512],
                                                                 start=(kc == 0), stop=(kc == 7)),
                       reads=[R["XNT"][kc], wr], writes=[bank(bA)])
                  S.pe(lambda e, w=w, j=j, kc=kc, bB=bB: e.matmul(PSF[:, bB, 0:160], lhsT=XNT[:, kc, j * 128:(j + 1) * 128], rhs=w[:, 512:672],
                                                                 start=(kc == 0), stop=(kc == 7)),
                       reads=[R["XNT"][kc], wr], writes=[bank(bB)])
          chk('s1b')
          for j in range(NSUB):
              bA, bB = 2 + 2 * j, 3 + 2 * j
              S.act(lambda e, bA=bA, j=j: e.activation(out=JUNK[:, 0:384], in_=PSF[:, bA, 0:384], func=AF.Square,
                                                       accum_out=SM[:, 8 + j:9 + j]), reads=[bank(bA)], writes=[R["JUNK"], R["SM"][6]])
              S.act(lambda e, bA=bA, j=j: e.activation(out=JUNK[:, 0:128], in_=PSF[:, bA, 384:512], func=AF.Square,
                                                       accum_out=SM[:, 10 + j:11 + j]), reads=[bank(bA)], writes=[R["JUNK"], R["SM"][6]])
              S.act(lambda e, bB=bB, j=j: e.activation(out=JUNK[:, 0:128], in_=PSF[:, bB, 0:128], func=AF.Square,
                                                       accum_out=SM[:, 12 + j:13 + j]), reads=[bank(bB)], writes=[R["JUNK"], R["SM"][6]])
              S.dve(lambda e, bA=bA, j=j: e.tensor_copy(out=QKA[:, j, 0:512], in_=PSF[:, bA, :]), reads=[bank(bA)], writes=[R["QKA"][j]])
              S.act(lambda e, bB=bB, j=j: e.activation(out=QKA[:, j, 512:640], in_=PSF[:, bB, 0:128], func=AF.Copy),
                    reads=[bank(bB)], writes=[R["QKA"][j]])
              S.dve(lambda e, bB=bB, j=j: e.tensor_copy(out=KR[:, j, :], in_=PSF[:, bB, 128:160]), reads=[bank(bB)], writes=[R["KR"][j]])
          chk('s1c')
          S.dve(lambda e: e.tensor_tensor(out=SM[:, 10:12], in0=SM[:, 10:12], in1=SM[:, 12:14], op=ALU.add),
                reads=[R["SM"][6]], writes=[R["SM"][6]])
          rstd_from_ss(SM[:, 8:10], SM[:, 16:18], 384.0, [R["SM"][6]], [R["SM"][7]])
          rstd_from_ss(SM[:, 10:12], SM[:, 18:20], 256.0, [R["SM"][6]], [R["SM"][8]])
          for j in range(NSUB):
              b = j
              for c in range(5):
                  S.pe(lambda e, j=j, c=c, b=b: e.transpose(out=PSB16[:, b, c * 128:(c + 1) * 128], in_=QKA[:, j, c * 128:(c + 1) * 128],
                                                            identity=IDB), reads=[R["QKA"][j], R["CONSTB"]], writes=[bank(b)])
              S.dve(lambda e, j=j, b=b: e.tensor_tensor(out=QAT[:, :, j * 128:(j + 1) * 128],
                                                        in0=PSB16[:, b, 0:640].rearrange("p (c t) -> p c t", c=5),
                                                        in1=bc(VEC[:, V_GQA:V_GQA + 5], [[1, 5], [0, 128]]), op=ALU.mult),
                    reads=[bank(b), R["VEC"]], writes=[R["QAT"]])
          if it == ntiles - 1:
              dump("qka", QKA, [R["QKA"]])
              dump("qat", QAT, [R["QAT"]])
              dump("kr", KR, [R["KR"]])
              dump("sm", SM, [R["SM"]])
          chk('s2')
          KVB = (6, 0)
          for kc in range(3):
              w, wr = piece(it, P_WQB + kc)
              for j in range(NSUB):
                  bq = 2 + 2 * j
                  S.pe(lambda e, w=w, j=j, kc=kc, bq=bq: e.matmul(PSF[:, bq, :], lhsT=QAT[:, kc, j * 128:(j + 1) * 128], rhs=w[:, 0:512],
                                                                 start=(kc == 0), stop=(kc == 2)), reads=[R["QAT"], wr], writes=[bank(bq)])
                  S.pe(lambda e, w=w, j=j, kc=kc, bq=bq: e.matmul(PSF[:, bq + 1, 0:256], lhsT=QAT[:, kc, j * 128:(j + 1) * 128], rhs=w[:, 512:768],
                                                                 start=(kc == 0), stop=(kc == 2)), reads=[R["QAT"], wr], writes=[bank(bq + 1)])
          for kc in range(2):
              w, wr = piece(it, P_WKVB + kc)
              for j in range(NSUB):
                  for hb in range(2):
                      S.pe(lambda e, w=w, j=j, kc=kc, hb=hb: e.matmul(PSF[:, KVB[j] + hb, :], lhsT=QAT[:, 3 + kc, j * 128:(j + 1) * 128],
                                                                     rhs=w[:, hb * 512:(hb + 1) * 512], start=(kc == 0), stop=(kc == 1)),
                           reads=[R["QAT"], wr], writes=[bank(KVB[j] + hb)])
          for j in range(NSUB):
              bq = 2 + 2 * j
              kb = KVB[j]
              blk = it * NSUB + j
              rq = SM[:, 16 + j:17 + j]
              rkv = SM[:, 18 + j:19 + j]
              qps = PSF[:, bq:bq + 2, :].rearrange("p b n -> p (b n)")[:, 0:768].rearrange("p (h d) -> p h d", h=8)
              kvps = PSF[:, kb:kb + 2, :].rearrange("p b n -> p (b n)").rearrange("p (h d) -> p h d", h=8)
              qrd = [bank(bq), bank(bq + 1)]
              kvrd = [bank(kb), bank(kb + 1)]
              cosb = bc(COS[:, blk, :], [[0, 8], [0, 2], [1, 16]])
              sinb = bc(SIN[:, blk, :], [[0, 8], [1, 16]])
              nsinb = bc(NSIN[:, blk, :], [[0, 8], [1, 16]])
              S.act(lambda e, qps=qps, rq=rq: e.activation(out=AUGQ[:, :, 0:64], in_=qps[:, :, 0:64], func=AF.Copy, scale=rq),
                    reads=qrd + [R["SM"][7]], writes=[R["AUGQ"]])
              S.dve(lambda e, qps=qps, rq=rq: e.tensor_scalar(out=QR, in0=qps[:, :, 64:96], scalar1=rq, scalar2=None, op0=ALU.mult),
                    reads=qrd + [R["SM"][7]], writes=[R["QR"]])
              S.act(lambda e, bq=bq, rq=rq: e.activation(out=SQ, in_=PSF[:, bq:bq + 2, :].rearrange("p b n -> p (b n)")[:, 0:768],
                                                         func=AF.Square, scale=rq), reads=qrd + [R["SM"][7]], writes=[R["SQ"]])
              S.dve(lambda e: e.tensor_reduce(out=NRM[:, 0:8], in_=SQ.rearrange("p (h d) -> p h d", h=8), axis=AX.X, op=ALU.add),
                    reads=[R["SQ"]], writes=[R["NRM"]])
              S.dve(lambda e: e.tensor_scalar(out=AUGQ[:, :, 96:97], in0=NRM[:, 0:8].unsqueeze(2), scalar1=-0.5, scalar2=None, op0=ALU.mult),
                    reads=[R["NRM"]], writes=[R["AUGQ"]])
              S.dve(lambda e, cosb=cosb: e.tensor_tensor(out=RA.rearrange("p h (a f) -> p h a f", a=2), in0=QR.rearrange("p h (a f) -> p h a f", a=2),
                                                         in1=cosb, op=ALU.mult), reads=[R["QR"], R["TAB"]], writes=[R["RA"]])
              S.dve(lambda e, nsinb=nsinb: e.tensor_tensor(out=RB[:, :, 0:16], in0=QR[:, :, 16:32], in1=nsinb, op=ALU.mult),
                    reads=[R["QR"], R["TAB"]], writes=[R["RB"]])
              S.dve(lambda e, sinb=sinb: e.tensor_tensor(out=RB[:, :, 16:32], in0=QR[:, :, 0:16], in1=sinb, op=ALU.mult),
                    reads=[R["QR"], R["TAB"]], writes=[R["RB"]])
              S.dve(lambda e: e.tensor_tensor(out=AUGQ[:, :, 64:96], in0=RA, in1=RB, op=ALU.add),
                    reads=[R["RA"], R["RB"]], writes=[R["AUGQ"]])
              S.act(lambda e, kvps=kvps, rkv=rkv: e.activation(out=AUGK[:, :, 0:64], in_=kvps[:, :, 0:64], func=AF.Copy, scale=rkv),
                    reads=kvrd + [R["SM"][8]], writes=[R["AUGK"]])
              S.act(lambda e, kvps=kvps, rkv=rkv: e.activation(out=VS, in_=kvps[:, :, 0:64], func=AF.Square, scale=rkv),
                    reads=kvrd + [R["SM"][8]], writes=[R["VS"]])
              S.dve(lambda e: e.tensor_reduce(out=NRM[:, 8:16], in_=VS, axis=AX.X, op=ALU.add), reads=[R["VS"]], writes=[R["NRM"]])
              S.act(lambda e, j=j: e.activation(out=JUNK[:, 0:32], in_=KR[:, j, :], func=AF.Square, accum_out=NRM[:, 16:17]),
                    reads=[R["KR"][j]], writes=[R["JUNK"], R["NRM"]])
              cos1 = bc(COS[:, blk, :], [[0, 2], [1, 16]])
              S.dve(lambda e, j=j, cos1=cos1: e.tensor_tensor(out=QR[:, 0, :].rearrange("p (a f) -> p a f", a=2),
                                                              in0=KR[:, j, :].rearrange("p (a f) -> p a f", a=2), in1=cos1, op=ALU.mult),
                    reads=[R["KR"][j], R["TAB"], R["AUGQ"]], writes=[R["QR"]])
              S.dve(lambda e, j=j, blk=blk: e.tensor_tensor(out=RB[:, 0, 0:16], in0=KR[:, j, 16:32], in1=NSIN[:, blk, :], op=ALU.mult),
                    reads=[R["KR"][j], R["TAB"], R["AUGQ"]], writes=[R["RB"]])
              S.dve(lambda e, j=j, blk=blk: e.tensor_tensor(out=RB[:, 0, 16:32], in0=KR[:, j, 0:16], in1=SIN[:, blk, :], op=ALU.mult),
                    reads=[R["KR"][j], R["TAB"]], writes=[R["RB"]])
              S.dve(lambda e: e.tensor_tensor(out=RA[:, 0, :], in0=QR[:, 0, :], in1=RB[:, 0, :], op=ALU.add),
                    reads=[R["QR"], R["RB"]], writes=[R["RA"]])
              S.dve(lambda e: e.tensor_copy(out=AUGK[:, :, 64:96], in_=bc(RA[:, 0, :], [[0, 8], [1, 32]])),
                    reads=[R["RA"]], writes=[R["AUGK"]])
              S.dve(lambda e: e.tensor_scalar(out=NRM[:, 8:16], in0=NRM[:, 8:16], scalar1=NRM[:, 16:17], scalar2=-0.5, op0=ALU.add, op1=ALU.mult),
                    reads=[R["NRM"]], writes=[R["NRM"]])
              S.dve(lambda e: e.tensor_copy(out=AUGK[:, :, 97:98], in_=NRM[:, 8:16].unsqueeze(2)), reads=[R["NRM"]], writes=[R["AUGK"]])
              S.act(lambda e: e.activation(out=NRM[:, 24:32].unsqueeze(2), in_=AUGK[:, :, 97:98], func=AF.Exp, scale=-ATT_SCALE),
                    reads=[R["AUGK"]], writes=[R["NRM"]])
              S.act(lambda e, kvps=kvps, rkv=rkv: e.activation(out=VS, in_=kvps[:, :, 64:128], func=AF.Copy, scale=rkv),
                    reads=kvrd + [R["SM"][8], R["NRM"]], writes=[R["VS"]])
              S.dve(lambda e, blk=blk: e.tensor_tensor(out=VP[:, blk, :, 0:64], in0=VS, in1=bc(NRM[:, 24:32], [[1, 8], [0, 64]]), op=ALU.mult),
                    reads=[R["VS"], R["NRM"]], writes=[R["VP"][blk]])
              S.dve(lambda e, blk=blk: e.tensor_copy(out=VP[:, blk, :, 64:65], in_=NRM[:, 24:32].unsqueeze(2)),
                    reads=[R["NRM"]], writes=[R["VP"][blk]])
              for h in range(8):
                  S.pe(lambda e, h=h, bq=bq: e.transpose(out=PSB16[0:98, bq, h * 128:(h + 1) * 128], in_=AUGK[:, h, :], identity=IDB),
                       reads=[R["AUGK"], R["CONSTB"]], writes=[bank(bq)])
              for h in range(8):
                  S.pe(lambda e, h=h, bq=bq: e.transpose(out=PSB16[0:98, bq + 1, h * 128:(h + 1) * 128], in_=AUGQ[:, h, :], identity=IDB),
                       reads=[R["AUGQ"], R["CONSTB"]], writes=[bank(bq + 1)])
              S.act(lambda e, blk=blk, bq=bq: e.activation(out=KT[:, :, blk * 128:(blk + 1) * 128],
                                                           in_=PSB16[0:98, bq, :].rearrange("p (h t) -> p h t", h=8), func=AF.Copy),
                    reads=[bank(bq)], writes=[R["KT"][blk]])
              S.dve(lambda e, j=j, bq=bq: e.tensor_copy(out=QT[:, :, j * 128:(j + 1) * 128], in_=PSB16[0:98, bq + 1, :].rearrange("p (h t) -> p h t", h=8)),
                    reads=[bank(bq + 1)], writes=[R["QT"]])
          if it == ntiles - 1:
              dump("qt", QT, [R["QT"]])
              dump("kt", KT[:, :, t0:t0 + T], [R["KT"]])
              dump("vp", VP[:, it * NSUB:(it + 1) * NSUB], [R["VP"]])
          chk('s3')
          S.dve(lambda e: e.memset(SM[:, 62:63], 0.0), writes=[R[n] for n in S3_RES + S4_RES] + [R["SM"][15]])
          for cg in range(4):
              wa, war = piece(it, P_WINF + 2 * cg)
              wg_, wgr = piece(it, P_WINF + 2 * cg + 1)
              ba, bg = 2 + 2 * (cg % 2), 3 + 2 * (cg % 2)
              for kc in range(8):
                  S.pe(lambda e, wa=wa, kc=kc, ba=ba: e.matmul(PSF[:, ba, 0:T], lhsT=wa[:, kc * 128:(kc + 1) * 128], rhs=XNT[:, kc, :],
                                                              start=(kc == 0), stop=(kc == 7)), reads=[R["XNT"][kc], war], writes=[bank(ba)])
              for kc in range(8):
                  S.pe(lambda e, wg_=wg_, kc=kc, bg=bg: e.matmul(PSF[:, bg, 0:T], lhsT=wg_[:, kc * 128:(kc + 1) * 128], rhs=XNT[:, kc, :],
                                                                start=(kc == 0), stop=(kc == 7)), reads=[R["XNT"][kc], wgr], writes=[bank(bg)])
              S.act(lambda e, cg=cg, bg=bg: e.activation(out=SIG[:, cg % 2, :], in_=PSF[:, bg, 0:T], func=AF.Sigmoid),
                    reads=[bank(bg)], writes=[R["SIG"][cg % 2]])
              S.dve(lambda e, cg=cg, ba=ba: e.tensor_tensor(out=GLU[:, cg, 30:30 + T], in0=PSF[:, ba, 0:T], in1=SIG[:, cg % 2, :], op=ALU.mult),
                    reads=[bank(ba), R["SIG"][cg % 2]], writes=[R["GLU"][cg]])
          conv_pending = []
          for jt in range(31):
              for cg in range(4):
                  wcol = VEC[:, V_CW + cg * 31 + jt:V_CW + cg * 31 + jt + 1]
                  if jt == 0:
                      conv_pending.append(lambda cg=cg, jt=jt, wcol=wcol: S.dve(
                          lambda e: e.tensor_scalar(out=ACC[:, cg, :], in0=GLU[:, cg, jt:jt + T], scalar1=wcol, scalar2=None, op0=ALU.mult),
                          reads=[R["GLU"][cg], R["VEC"]], writes=[R["ACC"][cg]], name="chain"))
                  else:
                      conv_pending.append(lambda cg=cg, jt=jt, wcol=wcol: S.dve(
                          lambda e: e.scalar_tensor_tensor(out=ACC[:, cg, :], in0=GLU[:, cg, jt:jt + T], scalar=wcol, in1=ACC[:, cg, :],
                                                           op0=ALU.mult, op1=ALU.add),
                          reads=[R["GLU"][cg], R["VEC"], R["ACC"][cg]], writes=[R["ACC"][cg]], name="chain"))
          conv_pending.reverse()

          def conv_some(n):
              for _ in range(n):
                  if conv_pending:
                      conv_pending.pop()()
          chk('s4')
          ln_state = [False]
          def emit_ln():
              for cg in range(4):
                  S.pool(lambda e, cg=cg: e.tensor_copy(out=GLU[:, cg, 0:30], in_=GLU[:, cg, T:T + 30]), reads=[R["GLU"][cg]], writes=[R["GLU"][cg]])
                  S.act(lambda e, cg=cg: e.activation(out=CBF[:, cg, :], in_=ACC[:, cg, :], func=AF.Identity, bias=VEC[:, V_CB + cg:V_CB + cg + 1]),
                        reads=[R["ACC"][cg], R["VEC"]], writes=[R["CBF"][cg]])
                  S.act(lambda e, cg=cg: e.activation(out=CSQ[:, cg, :], in_=ACC[:, cg, :], func=AF.Square, bias=VEC[:, V_CB + cg:V_CB + cg + 1]),
                        reads=[R["ACC"][cg], R["VEC"]], writes=[R["CSQ"][cg]])
              for cg in range(4):
                  S.pe(lambda e, cg=cg: e.matmul(PSF[:, 7, 0:T], lhsT=ONESB, rhs=CBF[:, cg, :], start=(cg == 0), stop=(cg == 3)),
                       reads=[R["CBF"][cg], R["CONSTB"]], writes=[bank(7)])
              for cg in range(4):
                  S.pe(lambda e, cg=cg: e.matmul(PSF[:, 6, 0:T], lhsT=ONESB, rhs=CSQ[:, cg, :], start=(cg == 0), stop=(cg == 3)),
                       reads=[R["CSQ"][cg], R["CONSTB"]], writes=[bank(6)])
              S.dve(lambda e: e.tensor_scalar(out=LNM, in0=PSF[:, 7, 0:T], scalar1=1.0 / 512, scalar2=None, op0=ALU.mult), reads=[bank(7)], writes=[R["LN"]])
              S.dve(lambda e: e.tensor_tensor(out=LNT[:, 0, :], in0=LNM, in1=LNM, op=ALU.mult), reads=[R["LN"]], writes=[R["LN"]])
              S.dve(lambda e: e.scalar_tensor_tensor(out=LNR, in0=PSF[:, 6, 0:T], scalar=1.0 / 512, in1=LNT[:, 0, :], op0=ALU.mult, op1=ALU.subtract),
                    reads=[bank(6), R["LN"]], writes=[R["LN"]])
              S.act(lambda e: e.activation(out=LNR, in_=LNR, func=AF.Ln, bias=SM[:, 48:49]), reads=[R["LN"], R["SM"][14]], writes=[R["LN"]])
              S.act(lambda e: e.activation(out=LNR, in_=LNR, func=AF.Exp, scale=-0.5), reads=[R["LN"]], writes=[R["LN"]])
              for cg in range(4):
                  S.dve(lambda e, cg=cg: e.scalar_tensor_tensor(out=LNT[:, cg % 2, :], in0=ACC[:, cg, :], scalar=VEC[:, V_CB + cg:V_CB + cg + 1], in1=LNM,
                                                                op0=ALU.add, op1=ALU.subtract),
                        reads=[R["ACC"][cg], R["LN"], R["VEC"]], writes=[R["LN"]])
                  S.dve(lambda e, cg=cg: e.tensor_tensor(out=LNT[:, cg % 2, :], in0=LNT[:, cg % 2, :], in1=LNR, op=ALU.mult), reads=[R["LN"]], writes=[R["LN"]])
                  S.act(lambda e, cg=cg: e.activation(out=CT[:, cg, :], in_=LNT[:, cg % 2, :], func=AF.Silu, scale=VEC[:, V_LNG + cg:V_LNG + cg + 1],
                                                      bias=VEC[:, V_LNB + cg:V_LNB + cg + 1]), reads=[R["LN"], R["VEC"]], writes=[R["CT"][cg]])
          conv_some(28)
          SSETS = (2, 0)
          units = []
          for h in range(8):
              kp = 0
              while kp + 1 < it:
                  units.append((h, [2 * kp, 2 * kp + 1, 2 * kp + 2, 2 * kp + 3], False, kp == 0))
                  kp += 2
              if kp < it:
                  units.append((h, [2 * kp, 2 * kp + 1], False, kp == 0))
              units.append((h, [2 * it, 2 * it + 1], True, it == 0))
          nun = len(units)

          def emit_qk(u):
              h, kts, band, first = units[u]
              sb0 = SSETS[u % 2]
              if not band:
                  for a_, kt in enumerate(kts):
                      sb_ = sb0 + a_ // 2
                      S.pe(lambda e, h=h, kt=kt, a_=a_, sb_=sb_: e.matmul(PSF[:, sb_, (a_ % 2) * T:(a_ % 2 + 1) * T], lhsT=KT[:, h, kt * 128:(kt + 1) * 128],
                                                                         rhs=QT[:, h, :], start=True, stop=True),
                           reads=[R["KT"][kt], R["QT"]], writes=[bank(sb_)])
              else:
                  sb_ = sb0
                  k0, k1 = kts
                  S.pe(lambda e, h=h, sb_=sb_, k0=k0: e.matmul(PSF[:, sb_, 0:T], lhsT=KT[:, h, k0 * 128:(k0 + 1) * 128], rhs=QT[:, h, :], start=True, stop=False),
                       reads=[R["KT"][k0], R["QT"]], writes=[bank(sb_)])
                  S.pe(lambda e, sb_=sb_: e.matmul(PSF[:, sb_, 0:128], lhsT=IDB, rhs=MASK, start=False, stop=True),
                       reads=[R["CONSTB"]], writes=[bank(sb_)])
                  S.pe(lambda e, h=h, sb_=sb_, k1=k1: e.matmul(PSF[:, sb_, 384:512], lhsT=KT[:, h, k1 * 128:(k1 + 1) * 128], rhs=QT[:, h, 128:256], start=True, stop=False),
                       reads=[R["KT"][k1], R["QT"]], writes=[bank(sb_)])
                  S.pe(lambda e, sb_=sb_: e.matmul(PSF[:, sb_, 384:512], lhsT=IDB, rhs=MASK, start=False, stop=True),
                       reads=[R["CONSTB"]], writes=[bank(sb_)])

          def emit_exp(u):
              h, kts, band, first = units[u]
              sb0 = SSETS[u % 2]
              pt, ptr = PT[u % 2], R["PT"][u % 2]
              if not band:
                  nb = len(kts) // 2
                  S.act(lambda e, sb0=sb0, pt=pt, nb=nb: e.activation(out=pt[:, 0:nb * 512], in_=PSF[:, sb0:sb0 + nb, :].rearrange("p b n -> p (b n)"),
                                                                      func=AF.Exp, scale=ATT_SCALE),
                        reads=[bank(sb0 + i) for i in range(nb)], writes=[ptr])
              else:
                  S.act(lambda e, sb0=sb0, pt=pt: e.activation(out=pt[:, 0:T], in_=PSF[:, sb0, 0:T], func=AF.Exp, scale=ATT_SCALE),
                        reads=[bank(sb0)], writes=[ptr])
                  S.act(lambda e, sb0=sb0, pt=pt: e.activation(out=pt[:, 384:512], in_=PSF[:, sb0, 384:512], func=AF.Exp, scale=ATT_SCALE),
                        reads=[bank(sb0)], writes=[ptr])

          def emit_pv(u):
              h, kts, band, first = units[u]
              ob = 4 + (h % 2)
              pt, ptr = PT[u % 2], R["PT"][u % 2]
              O_ps = PSF[0:65, ob, 0:T]
              if not band:
                  for a_, kt in enumerate(kts):
                      S.pe(lambda e, h=h, kt=kt, a_=a_, pt=pt, O_ps=O_ps, first=first: e.matmul(O_ps, lhsT=VP[:, kt, h, :], rhs=pt[:, a_ * T:(a_ + 1) * T],
                                                                                              start=(first and a_ == 0), stop=False),
                           reads=[R["VP"][kt], ptr], writes=[bank(ob)])
              else:
                  k0, k1 = kts
                  S.pe(lambda e, h=h, pt=pt, O_ps=O_ps, first=first, k0=k0: e.matmul(O_ps, lhsT=VP[:, k0, h, :], rhs=pt[:, 0:T], start=first, stop=False),
                       reads=[R["VP"][k0], ptr], writes=[bank(ob)])
                  S.pe(lambda e, h=h, pt=pt, ob=ob, k1=k1: e.matmul(PSF[0:65, ob, 128:256], lhsT=VP[:, k1, h, :], rhs=pt[:, 384:512], start=False, stop=True),
                       reads=[R["VP"][k1], ptr], writes=[bank(ob)])

          S.pool(lambda e: e.memset(RHL, 0.0), writes=[R["RDEN"]])

          def emit_norm_a(h):
              ob = 4 + (h % 2)
              S.act(lambda e, ob=ob: e.activation(out=RDEN[64:65, :], in_=PSF[64:65, ob, 0:T], func=AF.Ln), reads=[bank(ob)], writes=[R["RDEN"]])
              S.act(lambda e: e.activation(out=RDEN[64:65, :], in_=RDEN[64:65, :], func=AF.Exp, scale=-1.0), reads=[R["RDEN"]], writes=[R["RDEN"]])
              osb, osr = OS[h % 2], R["OS"][h % 2]
              S.act(lambda e, ob=ob, osb=osb: e.activation(out=osb, in_=PSF[0:64, ob, 0:T], func=AF.Copy), reads=[bank(ob)], writes=[osr])
              conv_some(8)
              S.dve(lambda e: e.tensor_copy(out=RHL[64:65, 0, :], in_=RDEN[64:65, :]), reads=[R["RDEN"]], writes=[R["RDEN"]])
              S.dve(lambda e: e.tensor_tensor(out=RHL[64:65, 1, :], in0=RDEN[64:65, :], in1=RHL[64:65, 0, :], op=ALU.subtract),
                    reads=[R["RDEN"]], writes=[R["RDEN"]])

          def emit_norm_b(h):
              osb, osr = OS[h % 2], R["OS"][h % 2]
              S.pe(lambda e: e.matmul(PSF[0:64, 6, 0:T], lhsT=SEL, rhs=RHL[:, 0, :], start=True, stop=False),
                   reads=[R["RDEN"], R["CONSTB"]], writes=[bank(6)])
              S.pe(lambda e: e.matmul(PSF[0:64, 6, 0:T], lhsT=SEL, rhs=RHL[:, 1, :], start=False, stop=True),
                   reads=[R["RDEN"], R["CONSTB"]], writes=[bank(6)])
              po = (h % 2) * 64
              conv_some(8)
              S.dve(lambda e, h=h, po=po, osb=osb: e.tensor_tensor(out=ATT[po:po + 64, h // 2, :], in0=osb, in1=PSF[0:64, 6, 0:T], op=ALU.mult),
                    reads=[osr, bank(6)], writes=[R["ATT"][h]])
              if not conv_pending and not ln_state[0]:
                  ln_state[0] = True
                  emit_ln()

          pend_a = []
          pend_b = []
          emit_qk(0)
          for u in range(nun):
              if u + 1 < nun:
                  emit_qk(u + 1)
              emit_exp(u)
              for hh, ue in [x for x in pend_b if x[1] < u]:
                  emit_norm_b(hh)
                  pend_b.remove((hh, ue))
              for hh in pend_a:
                  emit_norm_a(hh)
                  pend_b.append((hh, u))
              pend_a = []
              emit_pv(u)
              if units[u][2]:
                  pend_a.append(units[u][0])
          for hh, ue in pend_b:
              emit_norm_b(hh)
          for hh in pend_a:
              emit_norm_a(hh)
              emit_norm_b(hh)
          conv_some(1000)
          if not ln_state[0]:
              ln_state[0] = True
              emit_ln()
          if it == ntiles - 1:
              dump("glu", GLU, [R["GLU"]])
              dump("acc", ACC, [R["ACC"]])
              dump("ct", CT, [R["CT"]])
          if it == ntiles - 1:
              dump("att", ATT, [R["ATT"]])
          chk('s5')
          for kc in range(8):
              w, wr = piece(it, P_WO + kc)
              for j in range(NSUB):
                  lhs = ATT[:, kc, j * 128:(j + 1) * 128] if kc < 4 else CT[:, kc - 4, j * 128:(j + 1) * 128]
                  rd = [R["ATT"][2 * kc], R["ATT"][2 * kc + 1]] if kc < 4 else [R["CT"][kc - 4]]
                  for hb in range(2):
                      b = 4 * j + hb
                      S.pe(lambda e, w=w, lhs=lhs, hb=hb, b=b, kc=kc: e.matmul(PSF[:, b, :], lhsT=lhs, rhs=w[:, hb * 512:(hb + 1) * 512],
                                                                               start=(kc == 0), stop=(kc == 7)), reads=rd + [wr], writes=[bank(b)])
          for j in range(NSUB):
              post_norm_residual((4 * j, 4 * j + 1), j, 0)
          if it == ntiles - 1:
              dump("h1", H, [R["H"]])
          S.dve(lambda e: e.memset(SM[:, 60:61], 0.0), writes=[R[n] for n in MIXER_RES + FFN_RES] + [R["SM"][15]])
          chk('s6')
          norm_transpose(V_GFFN, (0, 1))
          ffn_pend = []
          gelu_pend = []
          for c in range(NFC):
              wg_, wgr = piece(it, P_WGU + 2 * c)
              wu_, wur = piece(it, P_WGU + 2 * c + 1)
              bg, bu = 2 + 2 * (c % 3), 3 + 2 * (c % 3)
              s2 = c % 3
              for kc in range(8):
                  S.pe(lambda e, wg_=wg_, kc=kc, bg=bg: e.matmul(PSF[:, bg, 0:T], lhsT=wg_[:, kc * 128:(kc + 1) * 128], rhs=XNT[:, kc, :],
                                                                start=(kc == 0), stop=(kc == 7)), reads=[R["XNT"][kc], wgr], writes=[bank(bg)])
              for kc in range(8):
                  S.pe(lambda e, wu_=wu_, kc=kc, bu=bu: e.matmul(PSF[:, bu, 0:T], lhsT=wu_[:, kc * 128:(kc + 1) * 128], rhs=XNT[:, kc, :],
                                                                start=(kc == 0), stop=(kc == 7)), reads=[R["XNT"][kc], wur], writes=[bank(bu)])
              gs, gsr = GS[s2], R["GS"][s2]
              a1, a1r = A1[s2], R["A1"][s2]
              a2, a2r = A2[s2], R["A2"][s2]
              S.act(lambda e, gs=gs, bg=bg: e.activation(out=gs[:, 2:2 + T], in_=PSF[:, bg, 0:T], func=AF.Copy),
                    reads=[bank(bg)], writes=[gsr])
              for f in gelu_pend:
                  f()
              gshr = R["GSH"][s2]
              S.pool(lambda e, gs=gs, c=c: e.tensor_copy(out=gs[:, 0:2], in_=GH[:, c, :]), reads=[R["GH"][c]], writes=[gshr])
              fw = VEC[:, V_FW + 3 * c:V_FW + 3 * c + 3]
              fb = VEC[:, V_FB + c:V_FB + c + 1]
              S.dve(lambda e, gs=gs, a1=a1, fw=fw, fb=fb: e.tensor_scalar(out=a1, in0=gs[:, 2:2 + T], scalar1=fw[:, 2:3], scalar2=fb, op0=ALU.mult, op1=ALU.add),
                    reads=[gsr, R["VEC"]], writes=[a1r], name="chain")
              S.dve(lambda e, gs=gs, a1=a1, fw=fw: e.scalar_tensor_tensor(out=a1, in0=gs[:, 1:1 + T], scalar=fw[:, 1:2], in1=a1, op0=ALU.mult, op1=ALU.add),
                    reads=[gsr, gshr, R["VEC"], a1r], writes=[a1r], name="chain")
              S.dve(lambda e, gs=gs, a1=a1, fw=fw: e.scalar_tensor_tensor(out=a1, in0=gs[:, 0:T], scalar=fw[:, 0:1], in1=a1, op0=ALU.mult, op1=ALU.add),
                    reads=[gsr, gshr, R["VEC"], a1r], writes=[a1r], name="chain")
              S.pool(lambda e, gs=gs, c=c: e.tensor_copy(out=GH[:, c, :], in_=gs[:, T:T + 2]), reads=[gsr], writes=[R["GH"][c]])
              for f in ffn_pend:
                  f()
              gelu_pend = [lambda a1=a1, a2=a2, a1r=a1r, a2r=a2r: S.act(
                  lambda e: e.activation(out=a2, in_=a1, func=AF.Gelu_apprx_tanh), reads=[a1r], writes=[a2r])]
              ffn_pend = [lambda a2=a2, bu=bu, c=c, a2r=a2r: S.dve(
                  lambda e: e.tensor_tensor(out=HID[:, c, :], in0=a2, in1=PSF[:, bu, 0:T], op=ALU.mult),
                  reads=[a2r, bank(bu)], writes=[R["HID"][c]])]
          for f in gelu_pend:
              f()
          for f in ffn_pend:
              f()
          for c in range(NFC):
              w, wr = piece(it, P_WD + c)
              for j in range(NSUB):
                  for hb in range(2):
                      b = 2 * j + hb
                      S.pe(lambda e, w=w, c=c, j=j, hb=hb, b=b: e.matmul(PSF[:, b, :], lhsT=HID[:, c, j * 128:(j + 1) * 128], rhs=w[:, hb * 512:(hb + 1) * 512],
                                                                         start=(c == 0), stop=(c == NFC - 1)), reads=[R["HID"][c], wr], writes=[bank(b)])
          for j in range(NSUB):
              post_norm_residual((2 * j, 2 * j + 1), j, 1)
          if it == ntiles - 1:
              dump("h2", H, [R["H"]])
          chk('s7')
          S.dve(lambda e: e.tensor_copy(out=PBF, in_=PS_), reads=[R["PSB"]], writes=[R["PBF"]])
          for j in range(NSUB):
              for c in range(2):
                  S.pe(lambda e, j=j, c=c: e.transpose(out=PSB16[:, 4, (c * NSUB + j) * 128:(c * NSUB + j + 1) * 128], in_=PBF[:, j, c * 128:(c + 1) * 128],
                                                       identity=IDB), reads=[R["PBF"], R["CONSTB"]], writes=[bank(4)])
          S.act(lambda e: e.activation(out=PTT.rearrange("p c t -> p (c t)"), in_=PSB16[:, 4, 0:2 * T], func=AF.Copy), reads=[bank(4)], writes=[R["PTT"]])
          norm_transpose(None, (6, 7))
          EB = (0, 4)
          GB = (2, 6)
          for kc in range(2):
              w, wr = piece(it, P_WPP + kc)
              for j in range(NSUB):
                  for hb in range(2):
                      S.pe(lambda e, w=w, kc=kc, j=j, hb=hb: e.matmul(PSF[:, EB[j] + hb, :], lhsT=PTT[:, kc, j * 128:(j + 1) * 128], rhs=w[:, hb * 512:(hb + 1) * 512],
                                                                     start=(kc == 0), stop=(kc == 1)), reads=[R["PTT"], wr], writes=[bank(EB[j] + hb)])
          for kc in range(8):
              w, wr = piece(it, P_WPG + kc)
              for j in range(NSUB):
                  for hb in range(2):
                      S.pe(lambda e, w=w, kc=kc, j=j, hb=hb: e.matmul(PSF[:, GB[j] + hb, :], lhsT=XNT[:, kc, j * 128:(j + 1) * 128], rhs=w[:, hb * 512:(hb + 1) * 512],
                                                                     start=(kc == 0), stop=(kc == 7)), reads=[R["XNT"][kc], wr], writes=[bank(GB[j] + hb)])
          for j in range(NSUB):
              eb, gb = EB[j], GB[j]
              src = PSF[:, eb:eb + 2, :].rearrange("p b n -> p (b n)")
              S.act(lambda e, src=src, j=j: e.activation(out=JUNK, in_=src, func=AF.Square, accum_out=SM[:, 24 + j:25 + j]),
                    reads=[bank(eb), bank(eb + 1)], writes=[R["JUNK"], R["SM"][9]])
              rstd_from_ss(SM[:, 24 + j:25 + j], SM[:, 26 + j:27 + j], 1024.0, [R["SM"][9]], [R["SM"][10]])
              S.dve(lambda e, src=src, j=j: e.scalar_tensor_tensor(out=E, in0=src, scalar=SM[:, 26 + j:27 + j], in1=GROW[:, 2, :], op0=ALU.mult, op1=ALU.mult),
                    reads=[bank(eb), bank(eb + 1), R["SM"][10], R["GROW"]], writes=[R["E"]])
              S.act(lambda e, gb=gb: e.activation(out=SG, in_=PSF[:, gb:gb + 2, :].rearrange("p b n -> p (b n)"), func=AF.Sigmoid),
                    reads=[bank(gb), bank(gb + 1)], writes=[R["SG"]])
              S.dve(lambda e: e.tensor_tensor(out=E, in0=E, in1=SG, op=ALU.mult), reads=[R["E"], R["SG"]], writes=[R["E"]])
              S.dve(lambda e, j=j: e.tensor_tensor(out=H[:, j, :], in0=H[:, j, :], in1=E, op=ALU.add), reads=[R["H"][j], R["E"]], writes=[R["H"][j]])
              st = S.dma("pool", lambda e, t0=t0, j=j: e.dma_start(out=y_d[t0 + j * 128:t0 + (j + 1) * 128, :], in_=H[:, j, :]),
                         reads=[R["H"][j]], writes=[R["Y"]], key=dkey("ystore%d" % j))
              final_ops.append(st)
          S.dve(lambda e: e.memset(SM[:, 61:62], 0.0), writes=[R[n] for n in MIXER_RES + FFN_RES] + [R["SM"][15]])
    except _Stop:
        pass
    if not final_ops:
        final_ops.append(S.dma("sp", lambda e: e.dma_start(out=y_d[0:128, 0:256], in_=VEC), reads=[R["VEC"], R["H"], R["XNT"], R["WS"], R["KT"], R["VP"], R["QT"], R["CT"], R["ATT"], R["TAB"], R["POSF"], R["QAT"], R["QKA"]],
                               writes=[R["Y"]], key=dkey("ystore")))
    S.emit(final_wait_ops=final_ops[-2:] + [op for op in S.ops if op.is_dma and op.key.name.startswith("kdbg")])
    nc._sched = S
    nc._sbuf_used = sbuf_used
    return nc


_NC_CACHE = {}


def make_in_maps(inputs):
    x = np.asarray(inputs["x"], np.float32)
    p = np.asarray(inputs["p"], np.float32)[0]
    pos = np.asarray(inputs["positions"]).astype(np.int32)
    vec = pack_vecs(inputs)
    grow = np.ascontiguousarray(np.broadcast_to(
        np.stack([np.asarray(inputs["g_mix_post"][0], np.float32), np.asarray(inputs["g_ffn_post"][0], np.float32),
                  np.asarray(inputs["g_ple"][0], np.float32)])[None], (128, 3, 1024)))
    consts = make_consts()
    shared = {
        "w_in": np.ascontiguousarray(inputs["w_in"][0], dtype=np.float32),
        "w_q_b": np.ascontiguousarray(inputs["w_q_b"][0], dtype=np.float32),
        "w_kv_b": np.ascontiguousarray(inputs["w_kv_b"][0], dtype=np.float32),
        "w_o": np.ascontiguousarray(inputs["w_o"][0], dtype=np.float32),
        "w_ffn_gate": np.ascontiguousarray(inputs["w_ffn_gate"][0], dtype=np.float32),
        "w_ffn_up": np.ascontiguousarray(inputs["w_ffn_up"][0], dtype=np.float32),
        "w_ffn_down": np.ascontiguousarray(inputs["w_ffn_down"][0], dtype=np.float32),
        "w_ple_proj": np.ascontiguousarray(inputs["w_ple_proj"][0], dtype=np.float32),
        "w_ple_gate": np.ascontiguousarray(inputs["w_ple_gate"][0], dtype=np.float32),
        "vecs": vec, "grow": grow, "consts": consts,
    }
    maps = []
    for b in range(8):
        m = dict(shared)
        m["x"] = np.ascontiguousarray(x[b])
        m["p"] = np.ascontiguousarray(p[b])
        m["pos"] = np.ascontiguousarray(pos[b].reshape(32, 128))
        maps.append(m)
    return maps


def kernel(**inputs):
    if "nc" not in _NC_CACHE:
        _NC_CACHE["nc"] = build_nc(same_engine_sync=("dve", "act", "pool"))
    nc = _NC_CACHE["nc"]
    maps = make_in_maps(inputs)
    res = run_bass_kernel_spmd(nc, maps, core_ids=list(range(8)))
    out = np.stack([np.asarray(r["y"], np.float32) for r in res.results], axis=0)
    return out
```
